# Optimizing a Trainium2 kernel written in Bass

```python
import jax, jax.numpy as jnp
from jax import lax
import numpy as np

D_MODEL = 1024
BATCH = 8
SEQ = 2048
DEPTH = 2
DEC_BATCH = 128
DEC_SEQ = 8
PAST_LEN = 8192
PAGE_SIZE = 128

N_MIXERS = 2
POOL_WINDOWS = (2, 4, 8, 16)
N_POOL_GROUPS = 4
POOL_GROUP_DIM = D_MODEL // N_POOL_GROUPS
POOL_PREFIX = max(POOL_WINDOWS) - 1
N_HEADS = 16
N_KV_HEADS = 4
HEAD_DIM = 64
GROUP = N_HEADS // N_KV_HEADS
WINDOW = 128
ROPE_THETA = 10000.0
D_FF = 2816
Q_DIM = N_HEADS * HEAD_DIM
KV_DIM = N_KV_HEADS * HEAD_DIM
QKV_DIM = Q_DIM + 2 * KV_DIM
RMS_EPS = 1e-6
NEG_INF = -1e30

kernel_name = "hybrid_pool_swa_sink_macaron_step"


def rmsnorm(x, g):
    xf = x.astype(jnp.float32)
    y = xf * lax.rsqrt(jnp.mean(xf * xf, axis=-1, keepdims=True) + RMS_EPS)
    return (y * g.astype(jnp.float32)).astype(x.dtype)


def half_ffn(x, g, w_in, w_out):
    h = rmsnorm(x, g) @ w_in
    gate, up = h[..., :D_FF], h[..., D_FF:]
    return x + 0.5 * ((jax.nn.silu(gate) * up) @ w_out)


def pool_mix(u, prefix, start_pos, w, scale):
    T = u.shape[1]
    P = POOL_PREFIX
    ext = jnp.concatenate([prefix.astype(u.dtype), u], axis=1)
    c = jnp.cumsum(ext.astype(jnp.float32), axis=1)
    c = jnp.pad(c, ((0, 0), (1, 0), (0, 0)))
    n_seen = start_pos + jnp.arange(T) + 1
    outs = []
    for gi, wg in enumerate(POOL_WINDOWS):
        sl = slice(gi * POOL_GROUP_DIM, (gi + 1) * POOL_GROUP_DIM)
        s = c[:, P + 1:P + T + 1, sl] - c[:, P + 1 - wg:P + T + 1 - wg, sl]
        cnt = jnp.minimum(n_seen, wg).astype(jnp.float32)[:, None]
        p = s / cnt - u[..., sl].astype(jnp.float32)
        outs.append(jnp.einsum("btc,cd->btd", p, w[gi].astype(jnp.float32)))
    y = jnp.concatenate(outs, axis=-1) * scale.astype(jnp.float32)
    return y.astype(u.dtype), ext[:, -P:]


def rope(x, pos):
    half = HEAD_DIM // 2
    inv = ROPE_THETA ** (-jnp.arange(half, dtype=jnp.float32) / half)
    ang = pos.astype(jnp.float32)[:, None] * inv[None, :]
    cos = jnp.cos(ang)[:, None, :]
    sin = jnp.sin(ang)[:, None, :]
    xf = x.astype(jnp.float32)
    x1, x2 = xf[..., :half], xf[..., half:]
    return jnp.concatenate([x1 * cos - x2 * sin, x2 * cos + x1 * sin], axis=-1).astype(x.dtype)


def project_qkv(u, pos, w_qkv, b_qkv):
    h = u @ w_qkv + b_qkv
    lead = u.shape[:-1]
    q = h[..., :Q_DIM].reshape(*lead, N_HEADS, HEAD_DIM)
    k = h[..., Q_DIM:Q_DIM + KV_DIM].reshape(*lead, N_KV_HEADS, HEAD_DIM)
    v = h[..., Q_DIM + KV_DIM:].reshape(*lead, N_KV_HEADS, HEAD_DIM)
    return rope(q, pos), rope(k, pos), v


def band_mask(qpos, kpos):
    d = qpos[..., :, None] - kpos[..., None, :]
    return (d >= 0) & (d < WINDOW) & (kpos[..., None, :] >= 0)


def attend_with_sinks(q, k, v, mask, sinks):
    s = jnp.einsum("...qhgd,...khd->...hgqk", q, k).astype(jnp.float32) * (HEAD_DIM ** -0.5)
    s = jnp.where(mask, s, NEG_INF)
    sink = sinks.astype(jnp.float32).reshape(N_KV_HEADS, GROUP, 1, 1)
    m = jnp.maximum(jnp.max(s, axis=-1, keepdims=True), sink)
    e = jnp.exp(s - m)
    p = e / (jnp.sum(e, axis=-1, keepdims=True) + jnp.exp(sink - m))
    return jnp.einsum("...hgqk,...khd->...qhgd", p.astype(v.dtype), v)


def swa_prompt(u, w_qkv, b_qkv, w_o, b_o, sinks):
    B, T, _ = u.shape
    nb = T // WINDOW
    pos = jnp.arange(T)
    q, k, v = project_qkv(u, pos, w_qkv, b_qkv)
    qb = q.reshape(B, nb, WINDOW, N_KV_HEADS, GROUP, HEAD_DIM)
    pad = jnp.zeros((B, WINDOW, N_KV_HEADS, HEAD_DIM), k.dtype)
    kb = jnp.concatenate([pad, k], axis=1).reshape(B, nb + 1, WINDOW, N_KV_HEADS, HEAD_DIM)
    vb = jnp.concatenate([pad, v], axis=1).reshape(B, nb + 1, WINDOW, N_KV_HEADS, HEAD_DIM)
    kband = jnp.concatenate([kb[:, :-1], kb[:, 1:]], axis=2)
    vband = jnp.concatenate([vb[:, :-1], vb[:, 1:]], axis=2)
    qpos = pos.reshape(nb, WINDOW)
    kpos = jnp.arange(nb)[:, None] * WINDOW - WINDOW + jnp.arange(2 * WINDOW)[None, :]
    mask = band_mask(qpos, kpos)[None, :, None, None]
    o = attend_with_sinks(qb, kband, vband, mask, sinks).reshape(B, T, Q_DIM)
    return o @ w_o + b_o, k[:, -WINDOW:], v[:, -WINDOW:]


def swa_sample(u, k_buf, v_buf, w_qkv, b_qkv, w_o, b_o, sinks):
    B, T, _ = u.shape
    pos = PAST_LEN + jnp.arange(T)
    q, k, v = project_qkv(u, pos, w_qkv, b_qkv)
    kk = jnp.concatenate([k_buf.astype(k.dtype), k], axis=1)
    vv = jnp.concatenate([v_buf.astype(v.dtype), v], axis=1)
    kpos = PAST_LEN - WINDOW + jnp.arange(WINDOW + T)
    mask = band_mask(pos, kpos)[None, None, None]
    qg = q.reshape(B, T, N_KV_HEADS, GROUP, HEAD_DIM)
    o = attend_with_sinks(qg, kk, vv, mask, sinks).reshape(B, T, Q_DIM)
    return o @ w_o + b_o, kk[:, -WINDOW:], vv[:, -WINDOW:]


def setup_inputs(seed: int = 0) -> dict:
    key = jax.random.key(seed)
    ks = jax.random.split(key, 24)
    f32 = jnp.float32
    n_pool = (DEPTH + 1) // 2
    n_attn = DEPTH // 2
    nrm = lambda k, s: jax.random.normal(k, s, f32)
    return {
        "x_prompt": nrm(ks[0], (BATCH, SEQ, D_MODEL)),
        "x_sample": nrm(ks[1], (DEC_BATCH, DEC_SEQ, D_MODEL)),
        "state_pool": nrm(ks[2], (n_pool, DEC_BATCH, POOL_PREFIX, D_MODEL)),
        "cache_k": nrm(ks[3], (n_attn, DEC_BATCH, WINDOW, N_KV_HEADS, HEAD_DIM)),
        "cache_v": nrm(ks[4], (n_attn, DEC_BATCH, WINDOW, N_KV_HEADS, HEAD_DIM)),
        "norm_ffn1": 1.0 + 0.05 * nrm(ks[5], (DEPTH, D_MODEL)),
        "ffn1_w_in": nrm(ks[6], (DEPTH, D_MODEL, 2 * D_FF)) * D_MODEL ** -0.5,
        "ffn1_w_out": nrm(ks[7], (DEPTH, D_FF, D_MODEL)) * D_FF ** -0.5,
        "norm_mix": 1.0 + 0.05 * nrm(ks[8], (DEPTH, D_MODEL)),
        "norm_ffn2": 1.0 + 0.05 * nrm(ks[9], (DEPTH, D_MODEL)),
        "ffn2_w_in": nrm(ks[10], (DEPTH, D_MODEL, 2 * D_FF)) * D_MODEL ** -0.5,
        "ffn2_w_out": nrm(ks[11], (DEPTH, D_FF, D_MODEL)) * D_FF ** -0.5,
        "pool_w": nrm(ks[12], (n_pool, N_POOL_GROUPS, POOL_GROUP_DIM, POOL_GROUP_DIM)) * POOL_GROUP_DIM ** -0.5,
        "pool_scale": 1.0 + 0.05 * nrm(ks[13], (n_pool, D_MODEL)),
        "attn_w_qkv": nrm(ks[14], (n_attn, D_MODEL, QKV_DIM)) * D_MODEL ** -0.5,
        "attn_b_qkv": 0.02 * nrm(ks[15], (n_attn, QKV_DIM)),
        "attn_w_o": nrm(ks[16], (n_attn, Q_DIM, D_MODEL)) * Q_DIM ** -0.5,
        "attn_b_o": 0.02 * nrm(ks[17], (n_attn, D_MODEL)),
        "attn_sinks": nrm(ks[18], (n_attn, N_HEADS)),
        "norm_final": 1.0 + 0.05 * nrm(ks[19], (D_MODEL,)),
    }


def reference(x_prompt, x_sample, state_pool, cache_k, cache_v,
              norm_ffn1, ffn1_w_in, ffn1_w_out, norm_mix, norm_ffn2, ffn2_w_in, ffn2_w_out,
              pool_w, pool_scale, attn_w_qkv, attn_b_qkv, attn_w_o, attn_b_o, attn_sinks,
              norm_final):
    xp, xs = x_prompt, x_sample
    pool_p, pool_s, kp_l, vp_l, ks_l, vs_l = [], [], [], [], [], []
    for i in range(DEPTH):
        xp = half_ffn(xp, norm_ffn1[i], ffn1_w_in[i], ffn1_w_out[i])
        xs = half_ffn(xs, norm_ffn1[i], ffn1_w_in[i], ffn1_w_out[i])
        up = rmsnorm(xp, norm_mix[i])
        us = rmsnorm(xs, norm_mix[i])
        j = i // N_MIXERS
        if i % N_MIXERS == 0:
            zeros = jnp.zeros((up.shape[0], POOL_PREFIX, D_MODEL), up.dtype)
            yp, sp = pool_mix(up, zeros, 0, pool_w[j], pool_scale[j])
            ys, ss = pool_mix(us, state_pool[j], PAST_LEN, pool_w[j], pool_scale[j])
            pool_p.append(sp)
            pool_s.append(ss)
        else:
            yp, kp, vp = swa_prompt(up, attn_w_qkv[j], attn_b_qkv[j], attn_w_o[j], attn_b_o[j], attn_sinks[j])
            ys, kn, vn = swa_sample(us, cache_k[j], cache_v[j], attn_w_qkv[j], attn_b_qkv[j],
                                    attn_w_o[j], attn_b_o[j], attn_sinks[j])
            kp_l.append(kp)
            vp_l.append(vp)
            ks_l.append(kn)
            vs_l.append(vn)
        xp = xp + yp
        xs = xs + ys
        xp = half_ffn(xp, norm_ffn2[i], ffn2_w_in[i], ffn2_w_out[i])
        xs = half_ffn(xs, norm_ffn2[i], ffn2_w_in[i], ffn2_w_out[i])
    y_prompt = rmsnorm(xp, norm_final)
    y_sample = rmsnorm(xs, norm_final)
    new_pool_prompt = jnp.stack(pool_p)
    new_pool_sample = jnp.stack(pool_s)
    new_k_prompt = jnp.stack(kp_l)
    new_v_prompt = jnp.stack(vp_l)
    new_k_sample = jnp.stack(ks_l)
    new_v_sample = jnp.stack(vs_l)
    return (y_prompt, y_sample, new_pool_prompt, new_pool_sample,
            new_k_prompt, new_v_prompt, new_k_sample, new_v_sample)
```

```python
import numpy as np
from contextlib import ExitStack
import concourse.bass as bass
import concourse.mybir as mybir
from concourse.bass_utils import run_bass_kernel_spmd

F32 = mybir.dt.float32
BF16 = mybir.dt.bfloat16
AF = mybir.ActivationFunctionType
ALU = mybir.AluOpType

COMPUTE = ("pe", "act", "dve", "pool")


class Res:
    __slots__ = ("name", "last_w", "readers")

    def __init__(self, name=""):
        self.name = name
        self.last_w = None
        self.readers = []


class DSem:
    __slots__ = ("sem", "count")

    def __init__(self, sem):
        self.sem = sem
        self.count = 0


class Op:
    __slots__ = ("eng", "fn", "deps", "needs_inc", "ms", "dsem", "dval", "idx")


class Prog:
    def __init__(self, nc, stack):
        self.nc = nc
        self.stack = stack
        self.ops = []
        self.esem = {}
        for e in COMPUTE:
            self.esem[e] = stack.enter_context(nc.semaphore("S_" + e))
        self.n_dsem = 0

    def dsem(self):
        self.n_dsem += 1
        return DSem(self.stack.enter_context(self.nc.semaphore("D%d" % self.n_dsem)))

    def alias(self, new_list, old_list):
        extra = []
        for old in old_list:
            extra.extend(old.readers)
            if old.last_w is not None:
                extra.append(old.last_w)
        for n in new_list:
            n.readers.extend(extra)

    def op(self, eng, fn, reads=(), writes=(), dsem=None):
        o = Op()
        o.eng = eng
        o.fn = fn
        o.needs_inc = False
        o.ms = None
        o.dsem = dsem
        o.dval = None
        o.idx = len(self.ops)
        if dsem is not None:
            dsem.count += 1
            o.dval = 16 * dsem.count
        deps = {}
        for r in reads:
            w = r.last_w
            if w is not None:
                deps[w.idx] = w
        for r in writes:
            w = r.last_w
            if w is not None:
                deps[w.idx] = w
            for rd in r.readers:
                deps[rd.idx] = rd
        latest = {}
        for d in deps.values():
            if d.dsem is None:
                if d.eng not in latest or d.idx > latest[d.eng].idx:
                    latest[d.eng] = d
        deps = {k: d for k, d in deps.items() if d.dsem is not None or latest[d.eng] is d}
        o.deps = []
        for d in deps.values():
            if d.dsem is None and d.eng == "pe" and eng == "pe" and dsem is None:
                continue
            o.deps.append(d)
            if d.dsem is None:
                d.needs_inc = True
        for r in reads:
            r.readers.append(o)
        for r in writes:
            r.last_w = o
            r.readers = []
        self.ops.append(o)
        return o

    def emit(self, final_waits=()):
        nc = self.nc
        cnt = {e: 0 for e in COMPUTE}
        for o in self.ops:
            if o.dsem is None and o.needs_inc:
                cnt[o.eng] += 1
                o.ms = cnt[o.eng]
        by_eng = {}
        for o in self.ops:
            by_eng.setdefault(o.eng, []).append(o)
        esem = self.esem

        def run(engname, engobj):
            seen_e = {e: 0 for e in COMPUTE}
            seen_d = {}
            for o in by_eng.get(engname, []):
                need_e = {}
                need_d = {}
                for d in o.deps:
                    if d.dsem is not None:
                        k = id(d.dsem)
                        if d.dval > seen_d.get(k, 0) and d.dval > need_d.get(k, (None, 0))[1]:
                            need_d[k] = (d.dsem, d.dval)
                    else:
                        if d.ms > seen_e[d.eng] and d.ms > need_e.get(d.eng, 0):
                            need_e[d.eng] = d.ms
                for e, v in need_e.items():
                    engobj.wait_ge(esem[e], v)
                    seen_e[e] = v
                for k, (ds, v) in need_d.items():
                    engobj.wait_ge(ds.sem, v)
                    seen_d[k] = v
                ins = o.fn(engobj)
                if o.dsem is not None:
                    ins.then_inc(o.dsem.sem, 16)
                elif o.needs_inc:
                    ins.then_inc(esem[o.eng], 1)
            if engname == "sp":
                for ds in final_waits:
                    if ds.count:
                        engobj.wait_ge(ds.sem, 16 * ds.count)

        with nc.Block() as block:
            @block.sync
            def _(e):
                run("sp", e)

            @block.gpsimd
            def _(e):
                run("pool", e)

            @block.tensor
            def _(e):
                run("pe", e)

            @block.scalar
            def _(e):
                run("act", e)

            @block.vector
            def _(e):
                run("dve", e)


D = 1024
T = 2176
NTP = 2048
TBS = [(0, 512), (512, 512), (1024, 512), (1536, 512), (2048, 128)]
DFF = 2816
GROUPS = [8, 8, 6]
NV = 92
C_NF1, C_NMIX, C_NF2, C_NFIN, C_PSC, C_BO, C_BQ = 0, 16, 32, 48, 56, 64, 72
EPS = 1e-6
STAGE = 99


def build_program():
    nc = bass.Bass("TRN2", target_bir_lowering=False)

    def din(n, s):
        return nc.dram_tensor(n, s, F32, kind="ExternalInput").ap()

    def dout(n, s):
        return nc.dram_tensor(n, s, F32, kind="ExternalOutput").ap()

    xp = din("xp", [2048, D])
    xs = din("xs", [128, D])
    spool = din("spool", [16, 15, D])
    ck = din("ck", [16, 128, 256])
    cv = din("cv", [16, 128, 256])
    wi = [din("w1i", [2, D, 2 * DFF]), din("w2i", [2, D, 2 * DFF])]
    wo = [din("w1o", [2, DFF, D]), din("w2o", [2, DFF, D])]
    pw = din("pw", [4, 256, 256])
    wqkv = din("wqkv", [D, 2816])
    wop = din("wop", [D, D])
    vecs_d = din("vecs", [128, NV])
    ident_d = din("ident", [128, 128])
    masks_d = din("masks", [128, 3, 128])
    maskc_d = din("maskc", [128, 8])
    cos_d = din("cosT", [128, T])
    sin_d = din("sinT", [128, T])
    invc_d = din("invc", [128, 4, 16])
    bvb_d = din("bvb", [128, 2, 128])
    sinks_d = din("sinks", [2, 16])
    sel_d = din("sel", [2, 2])
    perm_d = din("permm", [128, 128])

    yp = dout("yp", [2048, D])
    ys = dout("ys", [128, D])
    npp = dout("npp", [15, D])
    nps = dout("nps", [16, 15, D])
    nkp = dout("nkp", [128, 256])
    nvp = dout("nvp", [128, 256])
    nks = dout("nks", [16, 128, 256])
    nvs = dout("nvs", [16, 128, 256])

    with ExitStack() as st:
        P = Prog(nc, st)

        def sb(n, s, d):
            return st.enter_context(nc.sbuf_tensor("s_" + n, s, d))

        xT = sb("xT", [128, 8, T], F32)
        xn = sb("xn", [128, 8, T], BF16)
        scr = sb("scr", [128, 8704], F32)
        wout = sb("wout", [128, 8, 1024], BF16)
        win = [sb("win%d" % i, [128, 2, 8, 256], BF16) for i in range(2)]
        kT = sb("kT", [128, T], BF16)
        Vt = sb("Vt", [128, 17, 128], BF16)
        Pb = sb("Pb", [128, 4, 512], BF16)
        T32 = sb("T32", [128, 4, 512], F32)
        rs3 = sb("rs3", [128, 3, 512], F32)
        rstd = rs3[:, 0:2, :]
        rsq = rs3[:, 2, :]
        sq = sb("sq", [128, 2, 512], BF16)
        rope = sb("rope", [128, 2, 512], F32)
        vecs = sb("vecs", [128, NV], F32)
        identf = sb("identf", [128, 128], F32)
        identb = sb("identb", [128, 128], BF16)
        onesb = sb("onesb", [128, 128], BF16)
        skz = sb("skz", [128, 16], BF16)
        permb = sb("permb", [128, 128], BF16)
        masks = sb("masks", [128, 3, 128], BF16)
        maskc = sb("maskc", [128, 8], F32)
        invc = sb("invc", [128, 4, 16], F32)
        bvb = sb("bvb", [128, 2, 128], F32)
        cst = sb("cst", [128, 4], F32)
        sk16 = sb("sk16", [2, 16], F32)
        sklo = sb("sklo", [2, 16], F32)
        skhi = sb("skhi", [2, 16], BF16)
        skhl = sb("skhl", [2, 16], BF16)
        sel = sb("sel", [2, 2], F32)
        k32 = sb("k32", [128, 256], F32)
        vout = sb("vout", [128, 2, 128], F32)
        tmp16 = sb("tmp16", [128, 16], F32)
        psall = st.enter_context(nc.psum_tensor("psall", [128, 4096], F32))
        PS = [psall[:, i * 512:(i + 1) * 512] for i in range(8)]


        def MM(out, lhsT, rhs, start, stop, reads, writes):
            P.op("pe", lambda e: e.matmul(out, lhsT, rhs, start=start, stop=stop), reads, writes)

        R_cf = Res()

        def TR(out, in_, idn, reads, writes):
            P.op("pe", lambda e: e.transpose(out=out, in_=in_, identity=idn), list(reads) + [R_cf], writes)

        def ACT(out, in_, func, reads, writes, **kw):
            P.op("act", lambda e: e.activation(out=out, in_=in_, func=func, **kw), reads, writes)

        def TT(eng, out, in0, in1, op, reads, writes):
            P.op(eng, lambda e: e.tensor_tensor(out=out, in0=in0, in1=in1, op=op), reads, writes)

        def STT(out, in0, scalar, in1, op0, op1, reads, writes):
            P.op("dve", lambda e: e.scalar_tensor_tensor(out=out, in0=in0, scalar=scalar, in1=in1, op0=op0, op1=op1), reads, writes)

        def DMA(eng, out, in_, reads, writes, dsem):
            P.op(eng, lambda e: e.dma_start(out=out, in_=in_), reads, writes, dsem=dsem)

        def RECIP(out, in_, reads, writes):
            P.op("dve", lambda e: e.reciprocal(out=out, in_=in_), reads, writes)

        def VCOPY(out, in_, reads, writes):
            P.op("dve", lambda e: e.tensor_copy(out=out, in_=in_), reads, writes)

        hT = scr[:].bitcast(BF16).rearrange("p (c t) -> p c t", c=8)

        R_x = [[Res() for _ in TBS] for _ in range(8)]
        R_xn = [[Res() for _ in TBS] for _ in range(8)]
        R_h = [[Res() for _ in TBS] for _ in range(8)]
        R_ps = [Res() for _ in range(8)]
        R_win = [Res(), Res()]
        R_wout = Res()
        R_T32 = [Res() for _ in range(4)]
        R_rstd = [Res(), Res()]
        R_rsq = Res()
        R_sq = [Res(), Res()]
        R_rope = Res()
        R_const = Res()
        R_Pb = [Res() for _ in range(4)]
        R_kT = [Res() for _ in TBS]
        R_V = [Res() for _ in range(17)]
        R_k32 = Res()
        R_kout = Res()
        R_vout = Res()
        R_tmp16 = Res()
        R_sink = Res()
        allh = [r for row in R_h for r in row]
        allxn = [r for row in R_xn for r in row]

        D_win = [P.dsem(), P.dsem()]
        D_wout = P.dsem()
        D_inA, D_inB, D_inS, D_inA2, D_inB2 = P.dsem(), P.dsem(), P.dsem(), P.dsem(), P.dsem()
        D_const = P.dsem()
        D_cf = P.dsem()
        D_constP = P.dsem()
        D_rope = P.dsem()
        D_stg = P.dsem()
        D_kstg = P.dsem()
        D_vc = P.dsem()
        D_out = [P.dsem(), P.dsem()]
        D_o2 = P.dsem()
        D_ko = P.dsem()
        D_vo = P.dsem()
        D_d2d = P.dsem()

        DMA("pool", masks[:], masks_d, [], [R_const], D_constP)
        DMA("sp", identf[:], ident_d, [], [R_cf], D_cf)
        for t, d in [(vecs, vecs_d), (maskc, maskc_d),
                     (invc, invc_d), (bvb, bvb_d), (sk16, sinks_d), (sel, sel_d)]:
            DMA("sp", t[:], d, [], [R_const], D_const)
        DMA("pool", identb[:], ident_d, [], [R_const], D_constP)
        DMA("pool", permb[:], perm_d, [], [R_const], D_constP)
        P.op("dve", lambda e: e.memset(onesb[:], 1.0), writes=[R_const])
        P.op("dve", lambda e: e.memset(cst[:, 0:1], EPS), writes=[R_const])
        ACT(sk16[:], sk16[:], AF.Exp, [R_const], [R_sink])
        VCOPY(skhi[:], sk16[:], [R_sink], [R_sink])
        TT("dve", sklo[:], sk16[:], skhi[:], ALU.subtract, [R_sink], [R_sink])
        P.op("dve", lambda e: e.tensor_scalar(out=sklo[:], in0=sklo[:], scalar1=sel[:, 1:2], scalar2=None, op0=ALU.mult), [R_sink, R_const], [R_sink])
        STT(skhl[:], skhi[:], sel[:, 0:1], sklo[:], ALU.mult, ALU.add, [R_sink, R_const], [R_sink])
        P.op("dve", lambda e: e.memset(skz[:], 0.0), [], [R_sink])
        VCOPY(skz[0:2, :], skhl[:], [R_sink], [R_sink])

        slot_ctr = [0]

        def load_win_unit(Wi, u):
            s = slot_ctr[0] % 2
            slot_ctr[0] += 1
            for gu in range(2):
                c0 = gu * DFF + u * 256
                DMA("pool", win[s][:, gu, :, :], Wi[:, c0:c0 + 256].rearrange("(k p) n -> p k n", p=128),
                    [], [R_win[s]], D_win[s])
            return s

        def load_wout(Wo, c0, n):
            DMA("pool", wout[:, 0:n, :], Wo[c0 * 128:(c0 + n) * 128, :].rearrange("(c p) n -> p c n", p=128),
                [], [R_wout], D_wout)

        nctr = [0]

        def rstd_tb(tb):
            t0, n = TBS[tb]
            i = nctr[0] % 2
            nctr[0] += 1
            for c in range(8):
                j = c % 2
                ACT(sq[:, j, 0:n], xT[:, c, t0:t0 + n], AF.Square, [R_x[c][tb]], [R_sq[j]])
                MM(PS[6][:, 0:n], onesb[:], sq[:, j, 0:n], c == 0, c == 7, [R_sq[j], R_const], [R_ps[6]])
            ACT(rsq[:, 0:n], PS[6][:, 0:n], AF.Ln, [R_ps[6], R_const], [R_rsq], bias=cst[:, 0:1], scale=1.0 / D)
            ACT(rstd[:, i, 0:n], rsq[:, 0:n], AF.Exp, [R_rsq], [R_rstd[i]], scale=-0.5)
            return i

        def norm_to_xn(tb, gcol):
            t0, n = TBS[tb]
            i = rstd_tb(tb)
            for c in range(8):
                STT(xn[:, c, t0:t0 + n], xT[:, c, t0:t0 + n], vecs[:, gcol + c:gcol + c + 1], rstd[:, i, 0:n],
                    ALU.mult, ALU.mult, [R_x[c][tb], R_rstd[i], R_const], [R_xn[c][tb]])

        pending = []

        def drain(k=None):
            while pending and (k is None or k > 0):
                pending.pop(0)()
                if k is not None:
                    k -= 1

        def norm_pieces(tb, gcol):
            t0, n = TBS[tb]
            holder = {}

            def p_sq(c):
                def f():
                    if c == 0:
                        holder["i"] = nctr[0] % 2
                        nctr[0] += 1
                    j = c % 2
                    ACT(sq[:, j, 0:n], xT[:, c, t0:t0 + n], AF.Square, [R_x[c][tb]], [R_sq[j]])
                    MM(PS[6][:, 0:n], onesb[:], sq[:, j, 0:n], c == 0, c == 7, [R_sq[j], R_const], [R_ps[6]])
                return f

            def p_rs():
                i = holder["i"]
                ACT(rsq[:, 0:n], PS[6][:, 0:n], AF.Ln, [R_ps[6], R_const], [R_rsq], bias=cst[:, 0:1], scale=1.0 / D)
                ACT(rstd[:, i, 0:n], rsq[:, 0:n], AF.Exp, [R_rsq], [R_rstd[i]], scale=-0.5)

            def p_st(c):
                def f():
                    i = holder["i"]
                    STT(xn[:, c, t0:t0 + n], xT[:, c, t0:t0 + n], vecs[:, gcol + c:gcol + c + 1], rstd[:, i, 0:n],
                        ALU.mult, ALU.mult, [R_x[c][tb], R_rstd[i], R_const], [R_xn[c][tb]])
                return f

            return [p_sq(c) for c in range(8)] + [p_rs] + [p_st(c) for c in range(8)]

        stA = scr[:, 0:8192].rearrange("p (i d) -> p i d", i=8)
        stB = xn[:].bitcast(F32).rearrange("p c t -> p (c t)")[:, 0:8192].rearrange("p (i d) -> p i d", i=8)
        stS = wout[:].bitcast(F32).rearrange("p c t -> p (c t)")[:, 0:1024]
        R_st = [Res() for _ in range(5)]
        for hh, (stv, dd) in enumerate([(stA, D_inA), (stA, D_inA2), (stB, D_inB), (stB, D_inB2)]):
            lo = (hh % 2) * 4
            DMA("sp", stv[:, lo:lo + 4, :], xp[hh * 512:(hh + 1) * 512, :].rearrange("(i p) d -> p i d", p=128), [], [R_st[hh]], dd)
        DMA("sp", stS, xs, [], [R_st[4]], D_inS)
        tctr = 0
        for tb in range(5):
            t0, n = TBS[tb]
            for c in range(8):
                b = 4 + (tctr % 4)
                for tt in range(n // 128):
                    ti = t0 // 128 + tt
                    if ti < 8:
                        src, rs = stA[:, ti, c * 128:(c + 1) * 128], R_st[ti // 4]
                    elif ti < 16:
                        src, rs = stB[:, ti - 8, c * 128:(c + 1) * 128], R_st[ti // 4]
                    else:
                        src, rs = stS[:, c * 128:(c + 1) * 128], R_st[4]
                    TR(PS[b][:, tt * 128:(tt + 1) * 128], src, identf[:], [rs], [R_ps[b]])
                if tctr % 2 == 0:
                    ACT(xT[:, c, t0:t0 + n], PS[b][:, 0:n], AF.Copy, [R_ps[b]], [R_x[c][tb]])
                else:
                    VCOPY(xT[:, c, t0:t0 + n], PS[b][:, 0:n], [R_ps[b]], [R_x[c][tb]])
                tctr += 1
        P.alias(allh, [R_st[0], R_st[1]])
        P.alias(allxn, [R_st[2], R_st[3]])
        P.alias([R_wout], [R_st[4]])
        DMA("sp", nks[:, 0:120, :], ck[:, 8:128, :], [], [], D_d2d)
        DMA("sp", nvs[:, 0:120, :], cv[:, 8:128, :], [], [], D_d2d)
        DMA("sp", nps[:, 0:7, :], spool[:, 8:15, :], [], [], D_d2d)

        actr = [0]
        bctr = [0]

        def ffn(l, f, gcol, prefetched, do_norm, post_tb, mid_hook=None):
            Wi = wi[f][l]
            Wo = wo[f][l]
            if do_norm:
                for tb in range(5):
                    norm_to_xn(tb, gcol)
            nunits = DFF // 256
            loaded = dict(prefetched)
            nxt = [max(loaded.keys()) + 1 if loaded else 0]

            def ensure(u):
                while nxt[0] <= u and nxt[0] < nunits:
                    loaded[nxt[0]] = load_win_unit(Wi, nxt[0])
                    nxt[0] += 1

            c0 = 0
            for gi, ng in enumerate(GROUPS):
                last = (gi == len(GROUPS) - 1)
                if gi == 0:
                    ensure(1)
                    load_wout(Wo, 0, ng)
                for u in range(c0 // 2, (c0 + ng) // 2):
                    ensure(u)
                    s = loaded[u]
                    for half in range(2):
                        cl = 2 * u + half - c0
                        for tb in range(5):
                            t0, n = TBS[tb]
                            ab = actr[0] % 2
                            actr[0] += 1
                            pg, pu = 2 * ab, 2 * ab + 1
                            for gu, pb in ((0, pg), (1, pu)):
                                for k in range(8):
                                    MM(PS[pb][:, 0:n], win[s][:, gu, k, half * 128:(half + 1) * 128], xn[:, k, t0:t0 + n],
                                       k == 0, k == 7, [R_win[s], R_xn[k][tb]], [R_ps[pb]])
                            ACT(T32[:, ab, 0:n], PS[pg][:, 0:n], AF.Silu, [R_ps[pg]], [R_T32[ab]])
                            TT("dve", hT[:, cl, t0:t0 + n], PS[pu][:, 0:n], T32[:, ab, 0:n], ALU.mult,
                               [R_ps[pu], R_T32[ab]], [R_h[cl][tb]])
                    ensure(u + 1)
                if last and mid_hook is not None:
                    mid_hook()
                for tb in range(5):
                    t0, n = TBS[tb]
                    for m in range(8):
                        pb = (4, 5, 7)[bctr[0] % 3]
                        bctr[0] += 1
                        for cl in range(ng):
                            MM(PS[pb][:, 0:n], wout[:, cl, m * 128:(m + 1) * 128], hT[:, cl, t0:t0 + n],
                               cl == 0, cl == ng - 1, [R_wout, R_h[cl][tb]], [R_ps[pb]])
                        STT(xT[:, m, t0:t0 + n], PS[pb][:, 0:n], 0.5, xT[:, m, t0:t0 + n], ALU.mult, ALU.add,
                            [R_ps[pb], R_x[m][tb]], [R_x[m][tb]])
                        drain(4)
                    if last and post_tb is not None:
                        pending.extend(post_tb(tb))
                if last:
                    drain()
                c0 += ng
                if gi + 1 < len(GROUPS):
                    load_wout(Wo, c0, GROUPS[gi + 1])

        def prefetch_first(l, f):
            Wi = wi[f][l]
            return {0: load_win_unit(Wi, 0), 1: load_win_unit(Wi, 1)}

        def pool_mixer(post_tb):
            gcol = C_NMIX + 0
            E = [scr[:, c * 528:c * 528 + 527] for c in range(8)]
            AB = [[scr[:, 4224 + (2 * s + q) * 528:4224 + (2 * s + q) * 528 + 527] for q in range(2)] for s in range(2)]
            stg = scr[:, 6336:6336 + 2048].rearrange("p (h d) -> p h d", h=2)
            R_E = [Res() for _ in range(8)]
            R_AB = [[Res(), Res()], [Res(), Res()]]
            R_stg = Res()
            newres = R_E + [R_AB[0][0], R_AB[0][1], R_AB[1][0], R_AB[1][1], R_stg]
            P.alias(newres, allh)
            pwt = wout[:].rearrange("p c t -> p (c t)")[:, 0:2048].rearrange("p (g i n) -> p g i n", g=4, i=2)
            DMA("pool", pwt, pw.rearrange("g (i p) n -> p g i n", p=128), [], [R_wout], D_wout)
            for h in range(2):
                DMA("sp", stg[0:120, h, :], spool[8 * h:8 * h + 8, :, :], [], [R_stg], D_stg)
            ustage = T32[:, 2:4, :].rearrange("p a b -> p (a b)")
            R_us = [R_T32[2], R_T32[3]]

            def out_transposes(srcs, out_ap, in_ap):
                for cb in range(2):
                    for cc in range(4):
                        c = cb * 4 + cc
                        TR(PS[7][:, cc * 128:(cc + 1) * 128], srcs[c], identf[:], [R_E[c], R_const], [R_ps[7]])
                    ACT(ustage[:, cb * 512:(cb + 1) * 512], PS[7][:, :], AF.Copy, [R_ps[7]], [R_us[cb]])
                DMA("sp", out_ap, in_ap, R_us, [], D_o2)

            prb = [(T32[:, 0, :], R_T32[0]), (T32[:, 1, :], R_T32[1]), (T32[:, 0, :], R_T32[0]), (T32[:, 1, :], R_T32[1]),
                   (rope[:, 0, :], R_rope)]

            def rstd_into(tb):
                t0, n = TBS[tb]
                buf, rbuf = prb[tb]
                for c in range(8):
                    j = c % 2
                    ACT(sq[:, j, 0:n], xT[:, c, t0:t0 + n], AF.Square, [R_x[c][tb]], [R_sq[j]])
                    MM(PS[6][:, 0:n], onesb[:], sq[:, j, 0:n], c == 0, c == 7, [R_sq[j], R_const], [R_ps[6]])
                ACT(rsq[:, 0:n], PS[6][:, 0:n], AF.Ln, [R_ps[6], R_const], [R_rsq], bias=cst[:, 0:1], scale=1.0 / D)
                ACT(buf[:, 0:n], rsq[:, 0:n], AF.Exp, [R_rsq], [rbuf], scale=-0.5)

            state = {}

            def geo(tb):
                if tb < 4:
                    return 527, (lambda ap, a, b: ap[:, a:b]), (lambda ap: ap)
                return 23, (lambda ap, a, b: ap[:, :, a:b]), (lambda ap: ap.rearrange("p (s t) -> p s t", s=16))

            def U(tb, c):
                t0, n = TBS[tb]
                L, sl, vw = geo(tb)
                buf, rbuf = prb[tb]
                if tb < 4:
                    Ec = E[c]
                    if tb == 0:
                        P.op("pool", lambda e, Ec=Ec: e.memset(Ec[:, 0:15], 0.0), writes=[R_E[c]])
                    else:
                        ACT(Ec[:, 0:15], Ec[:, 512:527], AF.Copy, [R_E[c]], [R_E[c]])
                else:
                    Ec = vw(E[c][:, 0:368])
                    for h in range(2):
                        TR(PS[7][:, h * 120:(h + 1) * 120], stg[0:120, h, c * 128:(c + 1) * 128], identf[0:120, 0:120],
                           [R_stg, R_const], [R_ps[7]])
                    ACT(Ec[:, :, 0:15], PS[7][:, 0:240].rearrange("p (s t) -> p s t", s=16), AF.Copy, [R_ps[7]], [R_E[c]])
                STT(sl(Ec, 15, L), vw(xT[:, c, t0:t0 + n]), vecs[:, gcol + c:gcol + c + 1], vw(buf[:, 0:n]), ALU.mult, ALU.mult,
                    [R_x[c][tb], rbuf, R_const], [R_E[c]])
                state[(tb, c)] = Ec

            def A(tb, c):
                L, sl, vw = geo(tb)
                g = c // 2
                s = c % 2
                eng = "dve" if c >= 6 else "pool"
                src, rsrc = state[(tb, c)], R_E[c]
                for k in range(g + 1):
                    d = 2 ** k
                    lo = 2 * d - 1
                    dst = AB[s][k % 2] if tb < 4 else vw(AB[s][k % 2][:, 0:368])
                    rdst = R_AB[s][k % 2]
                    TT(eng, sl(dst, lo, L), sl(src, lo, L), sl(src, lo - d, L - d), ALU.add, [rsrc], [rdst])
                    src, rsrc = dst, rdst
                state[("s", tb, c)] = (src, rsrc)

            def Pp(tb, c):
                t0, n = TBS[tb]
                L, sl, vw = geo(tb)
                g = c // 2
                w = 2 ** (g + 1)
                src, rsrc = state[("s", tb, c)]
                Ec = state[(tb, c)]
                STT(vw(xn[:, c, t0:t0 + n]), sl(src, 15, L), 1.0 / w, sl(Ec, 15, L), ALU.mult, ALU.subtract,
                    [rsrc, R_E[c]], [R_xn[c][tb]])
                if tb == 0:
                    TT("dve", tmp16[:], src[:, 15:31], invc[:, g, :], ALU.mult, [rsrc, R_const], [R_tmp16])
                    TT("dve", xn[:, c, 0:16], tmp16[:], Ec[:, 15:31], ALU.subtract, [R_tmp16, R_E[c]], [R_xn[c][tb]])

            def linmap(tb):
                t0, n = TBS[tb]
                for oc in range(8):
                    g = oc // 2
                    pb = 4 + (bctr[0] % 2)
                    bctr[0] += 1
                    for icl in range(2):
                        MM(PS[pb][:, 0:n], pwt[:, g, icl, (oc % 2) * 128:(oc % 2) * 128 + 128], xn[:, 2 * g + icl, t0:t0 + n],
                           icl == 0, icl == 1, [R_wout, R_xn[2 * g + icl][tb]], [R_ps[pb]])
                    STT(xT[:, oc, t0:t0 + n], PS[pb][:, 0:n], vecs[:, C_PSC + oc:C_PSC + oc + 1], xT[:, oc, t0:t0 + n],
                        ALU.mult, ALU.add, [R_ps[pb], R_x[oc][tb], R_const], [R_x[oc][tb]])
                    drain(3)
                pending.extend(post_tb(tb))

            rstd_into(0)
            seq = [(tb, c) for tb in range(5) for c in range(8)]
            for k, (tb, c) in enumerate(seq):
                if k >= 2:
                    Pp(*seq[k - 2])
                    if seq[k - 2][1] == 7:
                        linmap(seq[k - 2][0])
                if tb == 4 and c == 0:
                    out_transposes([E[cc][:, 399:527] for cc in range(8)], npp, ustage[113:128, :])
                U(tb, c)
                A(tb, c)
                if c == 5 and tb < 4:
                    rstd_into(tb + 1)
            Pp(*seq[-2])
            Pp(*seq[-1])
            linmap(4)
            ssrc = []
            for c in range(8):
                dstc = E[c][:, 384:512]
                P.op("pool", lambda e, dstc=dstc, c=c: e.tensor_copy(
                    out=dstc.rearrange("p (s t) -> p s t", s=16),
                    in_=E[c][:, 0:368].rearrange("p (s t) -> p s t", s=16)[:, :, 15:23]), [R_E[c]], [R_E[c]])
                ssrc.append(dstc)
            out_transposes(ssrc, nps[:, 7:15, :], ustage[:, :])
            drain()
            P.alias(allh, newres)

        def attn_early_loads(j, what="wk"):
            base = j * 1408
            kstg_ = Pb[:].rearrange("p a b -> p (a b)").rearrange("p (s k) -> p s k", s=16)
            if "w" in what:
                for s in range(2):
                    for gu in range(2):
                        c0 = base + s * 512 + gu * 256
                        DMA("pool", win[s][:, gu, :, :], wqkv[:, c0:c0 + 256].rearrange("(k p) n -> p k n", p=128),
                            [], [R_win[s]], D_win[s])
            if "k" in what:
                DMA("pool", kstg_, ck[:, :, j * 128:(j + 1) * 128].rearrange("s k c -> k s c"), [], R_Pb, D_kstg)

        def attn_mixer(post_tb):
            wreg = wout[:].rearrange("p c t -> p (c t)")
            kvw = wreg[:, 0:3072].rearrange("p (k n) -> p k n", k=8)
            KcT = wreg[:, 3072:5120].rearrange("p (s k) -> p s k", s=16)
            Vc = wreg[:, 5120:7168].rearrange("p (s k) -> p s k", s=16)
            kstg = Pb[:].rearrange("p a b -> p (a b)").rearrange("p (s k) -> p s k", s=16)
            R_kvw, R_KcT, R_Vc = Res(), Res(), Res()
            P.alias([R_kvw, R_KcT, R_Vc], [R_wout])
            R_q = R_h
            kT_lo = kT
            kT_hi = rs3[:].rearrange("p a b -> p (a b)").bitcast(BF16)[:, 0:T]
            P.alias(R_kT, R_rstd + [R_rsq])
            P.op("pool", lambda e: e.memset(kT_lo[64:128, :], 0.0), [], R_kT)
            P.op("pool", lambda e: e.memset(kT_hi[0:64, :], 0.0), [], R_kT)
            exb = rope[:].rearrange("p a b -> p (a b)").bitcast(BF16).rearrange("p (a b) -> p a b", a=4)
            R_ex = [Res() for _ in range(4)]
            sctr = 0
            dma_q = []
            psb = PS[7][:].bitcast(BF16)
            for j in range(2):
                base = j * 1408
                if j == 0:
                    DMA("pool", kvw, wqkv[:, base + 1024:base + 1408].rearrange("(k p) n -> p k n", p=128), [], [R_kvw], D_wout)
                else:
                    attn_early_loads(1, "k")
                DMA("pool", Vc, cv[:, :, j * 128:(j + 1) * 128].rearrange("s k c -> k s c"), [], [R_Vc], D_vc)
                qfin = []
                for tb in range(5):
                    t0, n = TBS[tb]
                    if tb == 0:
                        P.alias([R_rope], R_ex)
                    DMA("sp", rope[:, 0, 0:n], cos_d[:, t0:t0 + n], [], [R_rope], D_rope)
                    DMA("sp", rope[:, 1, 0:n], sin_d[:, t0:t0 + n], [], [R_rope], D_rope)
                    for ch in range(5):
                        ab = actr[0] % 2
                        actr[0] += 1
                        p1, p2 = 2 * ab, 2 * ab + 1
                        bcol = C_BQ + j * 10 + 2 * ch
                        rt = [R_T32[p1], R_T32[p2]]
                        if ch < 4:
                            s_, gu = ch // 2, ch % 2
                            for k in range(8):
                                MM(PS[p1][:, 0:n], win[s_][:, gu, k, 0:128], xn[:, k, t0:t0 + n], k == 0, k == 7,
                                   [R_win[s_], R_xn[k][tb]], [R_ps[p1]])
                            P.op("dve", lambda e, o_=sq[:, ab, 0:n], i_=PS[p1][:, 0:n], b_=vecs[:, bcol:bcol + 1]: e.tensor_scalar(
                                out=o_, in0=i_, scalar1=b_, scalar2=None, op0=ALU.add), [R_ps[p1], R_const], [R_sq[ab]])

                            def fin(p1=p1, p2=p2, ab=ab, bcol=bcol, ch=ch, t0=t0, n=n, tb=tb, rt=rt):
                                MM(PS[p2][:, 0:n], permb[:], sq[:, ab, 0:n], True, True, [R_sq[ab], R_const], [R_ps[p2]])
                                STT(T32[:, p1, 0:n], PS[p1][:, 0:n], vecs[:, bcol:bcol + 1], rope[:, 0, 0:n], ALU.add, ALU.mult,
                                    [R_ps[p1], R_rope, R_const], [R_T32[p1]])
                                TT("dve", T32[:, p2, 0:n], PS[p2][:, 0:n], rope[:, 1, 0:n], ALU.mult, [R_ps[p2], R_rope], [R_T32[p2]])
                                TT("pool", hT[:, ch, t0:t0 + n], T32[:, p1, 0:n], T32[:, p2, 0:n], ALU.add, rt, [R_q[ch][tb]])

                            if qfin:
                                qfin.pop(0)()
                            qfin.append(fin)
                        else:
                            while qfin:
                                qfin.pop(0)()
                            for which, pp in ((0, p1), (1, p2)):
                                for k in range(8):
                                    MM(PS[pp][:, 0:n], kvw[:, k, which * 128:(which + 1) * 128], xn[:, k, t0:t0 + n], k == 0, k == 7,
                                       [R_kvw, R_xn[k][tb]], [R_ps[pp]])
                            STT(T32[:, p1, 0:n], PS[p1][:, 0:n], vecs[:, bcol:bcol + 1], rope[:, 0, 0:n], ALU.add, ALU.mult,
                                [R_ps[p1], R_rope, R_const], [R_T32[p1]])
                            STT(T32[:, p2, 0:n], PS[p2][:, 0:n], vecs[:, bcol + 1:bcol + 2], rope[:, 1, 0:n], ALU.add, ALU.mult,
                                [R_ps[p2], R_rope, R_const], [R_T32[p2]])
                            TT("pool", kT_lo[0:64, t0:t0 + n], T32[0:64, p1, 0:n], T32[0:64, p2, 0:n], ALU.add, rt, [R_kT[tb]])
                            TT("pool", kT_hi[64:128, t0:t0 + n], T32[64:128, p1, 0:n], T32[64:128, p2, 0:n], ALU.add, rt, [R_kT[tb]])
                            if tb == 3:
                                TT("pool", k32[:, 0:128], T32[:, p1, 384:512], T32[:, p2, 384:512], ALU.add, rt, [R_k32])
                            if tb == 4:
                                TT("pool", k32[:, 128:256], T32[:, p1, 0:128], T32[:, p2, 0:128], ALU.add, rt, [R_k32])
                    for tt in range(n // 128):
                        ti = t0 // 128 + tt
                        pb = 4 + (bctr[0] % 2)
                        bctr[0] += 1
                        for k in range(8):
                            MM(PS[pb][:, 0:128], xn[:, k, ti * 128:(ti + 1) * 128], kvw[:, k, 256:384], k == 0, k == 7,
                               [R_kvw, R_xn[k][tb]], [R_ps[pb]])
                        TT("dve", Vt[:, ti, :], PS[pb][:, 0:128], bvb[:, j, :], ALU.add, [R_ps[pb], R_const], [R_V[ti]])
                        if ti >= 15:
                            TT("dve", vout[:, ti - 15, :], PS[pb][:, 0:128], bvb[:, j, :], ALU.add, [R_ps[pb], R_const], [R_vout])

                for half in range(2):
                    for ss in range(8):
                        TR(psb[:, ss * 128:(ss + 1) * 128], kstg[:, half * 8 + ss, :], identb[:], R_Pb + [R_const], [R_ps[7]])
                    ACT(KcT[:, half * 8:(half + 1) * 8, :], psb.rearrange("p (s k) -> p s k", s=8), AF.Copy, [R_ps[7]], [R_KcT])
                if j == 1:
                    pf_holder["pf"] = {}
                    Wi_n = wi[1][1]
                    for u_ in range(2):
                        dma_q.append(lambda u_=u_: pf_holder["pf"].__setitem__(u_, load_win_unit(Wi_n, u_)))
                if j == 0:
                    def ld_w(s_, gu_):
                        c0_ = 1408 + s_ * 512 + gu_ * 256
                        DMA("pool", win[s_][:, gu_, :, :], wqkv[:, c0_:c0_ + 256].rearrange("(k p) n -> p k n", p=128),
                            [], [R_win[s_]], D_win[s_])
                    for s_ in range(2):
                        for gu_ in range(2):
                            dma_q.append(lambda s_=s_, gu_=gu_: ld_w(s_, gu_))
                    dma_q.append(lambda: DMA("pool", kvw, wqkv[:, 1408 + 1024:1408 + 1408].rearrange("(k p) n -> p k n", p=128),
                                             [], [R_kvw], D_wout))
                for t in range(2):
                    TR(PS[7][:, t * 128:(t + 1) * 128], k32[:, t * 128:(t + 1) * 128], identf[:], [R_k32, R_const], [R_ps[7]])
                ACT(rsq[:, 256:512], PS[7][:, 0:256], AF.Copy, [R_ps[7]], [R_rsq])
                DMA("sp", nkp[:, j * 128:(j + 1) * 128], rsq[:, 256:384], [R_rsq], [], D_ko)
                DMA("sp", nks[:, 120:128, j * 128:(j + 1) * 128], rsq[:, 384:512], [R_rsq], [], D_ko)
                DMA("sp", nvp[:, j * 128:(j + 1) * 128], vout[:, 0, :], [R_vout], [], D_vo)
                DMA("sp", nvs[:, 120:128, j * 128:(j + 1) * 128], vout[:, 1, :], [R_vout], [], D_vo)
                v4 = lambda ap: ap.rearrange("p (h q) -> p h q", h=4)
                if j == 0:
                    blocks = [(gg, b) for gg in range(2) for b in range(17)]
                else:
                    blocks = [(0, 16), (1, 16)] + [(gg, b) for gg in range(2) for b in range(16)]
                P.alias(R_ex, [R_rope])

                def stageA(gg, b, ab, cur_eng="pool"):
                    rows = slice(gg * 64, gg * 64 + 64)
                    tb = min(b // 4, 4)
                    blk0 = b * 128
                    s0, s1 = 2 * ab, 2 * ab + 1
                    rq = [R_q[ch][tb] for ch in range(4)]
                    qv = hT[:, 0:4, blk0:blk0 + 128]
                    kTg = kT_lo if gg == 0 else kT_hi
                    MM(v4(PS[s0][:, :]), kTg[:, blk0:blk0 + 128], qv, True, True, rq + [R_kT[tb]], [R_ps[s0]])
                    have_prev = (0 < b < 16)
                    if have_prev:
                        MM(v4(PS[s1][:, :]), kTg[:, blk0 - 128:blk0], qv, True, True, rq + [R_kT[(b - 1) // 4]], [R_ps[s1]])
                    if b == 16:
                        for s in range(16):
                            MM(v4(PS[s1][:, s * 32:(s + 1) * 32]), KcT[rows, s, :], hT[rows, 0:4, 2048 + s * 8:2048 + s * 8 + 8],
                               True, True, rq + [R_KcT], [R_ps[s1]])
                    mi = 0 if b < 16 else 2
                    if have_prev or b == 16:
                        ACT(exb[:, s0:s0 + 2, :].rearrange("p a b -> p (a b)"), psall[:, s0 * 512:(s0 + 2) * 512], AF.Exp,
                            [R_ps[s0], R_ps[s1]], [R_ex[s0], R_ex[s1]], scale=0.125)
                    else:
                        ACT(exb[:, s0, :], PS[s0][:, :], AF.Exp, [R_ps[s0]], [R_ex[s0]], scale=0.125)
                    TT(cur_eng, v4(Pb[:, s0, :]), v4(exb[:, s0, :]), masks[:, mi, :].unsqueeze(1).broadcast_to([128, 4, 128]),
                       ALU.mult, [R_ex[s0], R_const], [R_Pb[s0]])
                    if have_prev or b == 16:
                        if b < 16:
                            TT("dve", v4(Pb[:, s1, :]), v4(exb[:, s1, :]), masks[:, 1, :].unsqueeze(1).broadcast_to([128, 4, 128]),
                               ALU.mult, [R_ex[s1], R_const], [R_Pb[s1]])
                        else:
                            TT("dve", Pb[:, s1, :].rearrange("p (a t) -> p a t", t=8), exb[:, s1, :].rearrange("p (a t) -> p a t", t=8),
                               maskc[:].unsqueeze(1).broadcast_to([128, 64, 8]), ALU.mult, [R_ex[s1], R_const], [R_Pb[s1]])

                def stageB(gg, b, ab):
                    g = 2 * j + gg
                    rows = slice(gg * 64, gg * 64 + 64)
                    tb = min(b // 4, 4)
                    blk0 = b * 128
                    s0, s1 = 2 * ab, 2 * ab + 1
                    po = 4 + ab
                    pd = 6 + ab
                    have_prev = (0 < b < 16)
                    two = have_prev or b == 16
                    MM(PS[po][:, :], Vt[:, b, :], Pb[:, s0, :], True, not two, [R_V[b], R_Pb[s0]], [R_ps[po]])
                    if have_prev:
                        MM(PS[po][:, :], Vt[:, b - 1, :], Pb[:, s1, :], False, True, [R_V[b - 1], R_Pb[s1]], [R_ps[po]])
                    if b == 16:
                        for s in range(16):
                            MM(v4(PS[po][:, :])[:, :, s * 8:(s + 1) * 8], Vc[:, s, :], v4(Pb[:, s1, s * 32:(s + 1) * 32]),
                               False, s == 15, [R_Vc, R_Pb[s1]], [R_ps[po]])
                    MM(v4(PS[pd][:, :]), onesb[:], skz[:, 4 * g:4 * g + 4].unsqueeze(2).broadcast_to([128, 4, 128]),
                       True, False, [R_sink, R_const], [R_ps[pd]])
                    MM(PS[pd][:, :], onesb[:], Pb[:, s0, :], False, not two, [R_Pb[s0], R_const], [R_ps[pd]])
                    if have_prev:
                        MM(PS[pd][:, :], onesb[:], Pb[:, s1, :], False, True, [R_Pb[s1], R_const], [R_ps[pd]])
                    if b == 16:
                        MM(PS[pd][:, :].rearrange("p (h s t) -> p h s t", h=4, s=16), onesb[:],
                           Pb[:, s1, :].rearrange("p (s h t) -> p h s t", s=16, h=4), False, True, [R_Pb[s1], R_const], [R_ps[pd]])
                    ACT(T32[rows, ab, :], PS[pd][rows, :], AF.Ln, [R_ps[pd]], [R_T32[ab]])
                    ACT(T32[rows, ab, :], T32[rows, ab, :], AF.Exp, [R_T32[ab]], [R_T32[ab]], scale=-1.0)
                    if j == 0:
                        odst, ores = hT[rows, 4:8, blk0:blk0 + 128], [R_h[4 + hh][tb] for hh in range(4)]
                    else:
                        odst, ores = xn[rows, 4:8, blk0:blk0 + 128], [R_xn[4 + hh][tb] for hh in range(4)]
                    TT("dve", odst, v4(PS[po][rows, :]), v4(T32[rows, ab, :]), ALU.mult, [R_ps[po], R_T32[ab]], ores)

                stageA(blocks[0][0], blocks[0][1], sctr % 2)
                dve_left = 0
                for bi in range(len(blocks)):
                    if dma_q and bi >= 4 and bi % 3 == 1:
                        dma_q.pop(0)()
                        dve_left = 2
                    if bi + 1 < len(blocks):
                        stageA(blocks[bi + 1][0], blocks[bi + 1][1], (sctr + 1) % 2, "dve" if dve_left > 0 else "pool")
                        dve_left -= 1
                    stageB(blocks[bi][0], blocks[bi][1], sctr % 2)
                    sctr += 1
                    if j == 1 and bi == 1:
                        dve_left = 2
                        P.alias([R_wout], [R_kvw, R_KcT, R_Vc])
                        DMA("pool", wout[:], wop.rearrange("(c p) n -> p c n", p=128), [], [R_wout], D_wout)
            while dma_q:
                dma_q.pop(0)()
            P.alias(R_rstd + [R_rsq], R_kT)
            for tb in range(5):
                t0, n = TBS[tb]
                for m in range(8):
                    pb = 4 + (bctr[0] % 2)
                    bctr[0] += 1
                    for c in range(8):
                        if c < 4:
                            src, rs = hT[:, 4 + c, t0:t0 + n], R_h[4 + c][tb]
                        else:
                            src, rs = xn[:, c, t0:t0 + n], R_xn[c][tb]
                        MM(PS[pb][:, 0:n], wout[:, c, m * 128:(m + 1) * 128], src, c == 0, c == 7, [R_wout, rs], [R_ps[pb]])
                    STT(xT[:, m, t0:t0 + n], PS[pb][:, 0:n], vecs[:, C_BO + m:C_BO + m + 1], xT[:, m, t0:t0 + n],
                        ALU.add, ALU.add, [R_ps[pb], R_x[m][tb], R_const], [R_x[m][tb]])
                    drain(3)
                pending.extend(post_tb(tb))
            drain()

        ysw = [win[h][:].bitcast(F32).rearrange("p a k n -> p (a k n)").rearrange("p (t d) -> p t d", t=2) for h in range(2)]
        fctr = [0]

        def final_pieces(tb):
            t0, n = TBS[tb]
            nt = n // 128
            pcs = norm_pieces(tb, 0)[:9]
            holder = {}

            def p_stt(c):
                def f():
                    i = (nctr[0] - 1) % 2
                    tq = c % 4
                    STT(T32[:, tq, 0:n], xT[:, c, t0:t0 + n], vecs[:, C_NFIN + c:C_NFIN + c + 1], rstd[:, i, 0:n],
                        ALU.mult, ALU.mult, [R_x[c][tb], R_rstd[i], R_const], [R_T32[tq]])
                return f

            def p_tr(cb):
                def f():
                    for cc in range(4):
                        tq = cc
                        for tt in range(nt):
                            TR(PS[cc][:, tt * 128:(tt + 1) * 128], T32[:, tq, tt * 128:(tt + 1) * 128], identf[:],
                               [R_T32[tq], R_const], [R_ps[cc]])
                return f

            def p_ev(cb, cc):
                def f():
                    c = cb * 4 + cc
                    for h in range((nt + 1) // 2):
                        w_ = min(2, nt - 2 * h)
                        src = PS[cc][:, h * 256:h * 256 + w_ * 128].rearrange("p (t d) -> p t d", t=w_)
                        dst = ysw[h][:, 0:w_, c * 128:(c + 1) * 128]
                        ACT(dst, src, AF.Copy, [R_ps[cc]], [R_win[h]])
                return f

            def p_out():
                if tb < 4:
                    for h in range(2):
                        DMA("sp", yp[t0 + h * 256:t0 + (h + 1) * 256, :].rearrange("(t p) d -> p t d", p=128), ysw[h],
                            [R_win[h]], [], D_out[h])
                else:
                    DMA("sp", ys, ysw[0][:, 0, :], [R_win[0]], [], D_out[0])

            seq_ = []
            for cb in range(2):
                seq_ += [p_stt(cb * 4 + cc) for cc in range(4)] + [p_tr(cb)] + [p_ev(cb, cc) for cc in range(4)]
            return pcs + seq_ + [p_out]

        pf = prefetch_first(0, 0)
        ffn(0, 0, C_NF1 + 0, pf, True, None)
        pf = prefetch_first(0, 1)
        pool_mixer(lambda tb: norm_pieces(tb, C_NF2 + 0))
        ffn(0, 1, C_NF2 + 0, pf, False, lambda tb: norm_pieces(tb, C_NF1 + 8))
        pf = prefetch_first(1, 0)
        ffn(1, 0, C_NF1 + 8, pf, False, lambda tb: norm_pieces(tb, C_NMIX + 8), mid_hook=lambda: attn_early_loads(0))
        pf_holder = {}
        attn_mixer(lambda tb: norm_pieces(tb, C_NF2 + 8))
        pf = pf_holder["pf"]
        ffn(1, 1, C_NF2 + 8, pf, False, final_pieces)
        P.emit(final_waits=[D_out[0], D_out[1], D_o2, D_d2d, D_ko, D_vo])
    return nc


def _host_consts():
    ident = np.eye(128, dtype=np.float32)
    k = np.arange(128)[:, None]
    q = np.arange(128)[None, :]
    masks = np.zeros((128, 3, 128), np.float32)
    masks[:, 0, :] = (k <= q)
    masks[:, 1, :] = (k > q)
    masks[:, 2, :] = ((k // 8) == (q // 8)) & ((k % 8) <= (q % 8))
    maskc = (np.arange(128)[:, None] > np.arange(8)[None, :]).astype(np.float32)
    half = 32
    inv = (np.float32(10000.0) ** (-np.arange(half, dtype=np.float32) / np.float32(half))).astype(np.float32)
    pos = np.concatenate([np.arange(2048), np.tile(8192 + np.arange(8), 16)]).astype(np.float32)
    ang = pos[:, None] * inv[None, :]
    cos = np.cos(ang).astype(np.float32)
    sin = np.sin(ang).astype(np.float32)
    d = np.arange(128) % 64
    cosT = cos[:, d % 32].T.copy()
    sgn = np.where(d < 32, -1.0, 1.0).astype(np.float32)
    sinT = (sin[:, d % 32].T * sgn[:, None]).astype(np.float32).copy()
    invc = np.zeros((128, 4, 16), np.float32)
    for g in range(4):
        w = 2 ** (g + 1)
        invc[:, g, :] = (1.0 / np.minimum(np.arange(16) + 1, w)).astype(np.float32)[None, :]
    return ident, masks, maskc, cosT, sinT, invc


def _fm(v):
    return np.ascontiguousarray(np.asarray(v, np.float32).reshape(8, 128).T)


_NC_CACHE = {}


def kernel(x_prompt, x_sample, state_pool, cache_k, cache_v,
           norm_ffn1, ffn1_w_in, ffn1_w_out, norm_mix, norm_ffn2, ffn2_w_in, ffn2_w_out,
           pool_w, pool_scale, attn_w_qkv, attn_b_qkv, attn_w_o, attn_b_o, attn_sinks,
           norm_final):
    f = lambda a: np.ascontiguousarray(np.asarray(a, dtype=np.float32))
    x_prompt, x_sample, state_pool = f(x_prompt), f(x_sample), f(state_pool)
    cache_k, cache_v = f(cache_k), f(cache_v)
    ident, masks, maskc, cosT, sinT, invc = _host_consts()

    wq = f(attn_w_qkv)[0]
    bq = f(attn_b_qkv)[0]
    perm64 = (np.arange(64) + 32) % 64
    cols = []
    bcols = []
    for j in range(2):
        for i in range(4):
            ha, hb = 8 * j + i, 8 * j + 4 + i
            o = np.concatenate([ha * 64 + np.arange(64), hb * 64 + np.arange(64)])
            pm = np.concatenate([ha * 64 + perm64, hb * 64 + perm64])
            cols += [o, pm]
            bcols += [o, pm]
        ko = 1024 + j * 128 + np.arange(128)
        kp = np.concatenate([1024 + (2 * j) * 64 + perm64, 1024 + (2 * j + 1) * 64 + perm64])
        cols += [ko, kp, 1280 + j * 128 + np.arange(128)]
        bcols += [ko, kp]
    wqkv_ext = np.ascontiguousarray(wq[:, np.concatenate(cols)])
    assert wqkv_ext.shape == (1024, 2816)
    rows = []
    for j in range(2):
        for i in range(4):
            ha, hb = 8 * j + i, 8 * j + 4 + i
            rows += [ha * 64 + np.arange(64), hb * 64 + np.arange(64)]
    wop = np.ascontiguousarray(f(attn_w_o)[0][np.concatenate(rows), :])

    vecs = np.zeros((128, NV), np.float32)
    for l in range(2):
        vecs[:, C_NF1 + l * 8:C_NF1 + l * 8 + 8] = _fm(norm_ffn1[l])
        vecs[:, C_NMIX + l * 8:C_NMIX + l * 8 + 8] = _fm(norm_mix[l])
        vecs[:, C_NF2 + l * 8:C_NF2 + l * 8 + 8] = _fm(norm_ffn2[l])
    vecs[:, C_NFIN:C_NFIN + 8] = _fm(norm_final)
    vecs[:, C_PSC:C_PSC + 8] = _fm(f(pool_scale)[0])
    vecs[:, C_BO:C_BO + 8] = _fm(f(attn_b_o)[0])
    for idx, bc in enumerate(bcols):
        vecs[:, C_BQ + idx] = bq[bc]
    bvb = np.ascontiguousarray(np.broadcast_to(bq[1280:1536].reshape(1, 2, 128), (128, 2, 128))).astype(np.float32)
    sinks = np.ascontiguousarray(np.broadcast_to(f(attn_sinks).reshape(1, 16), (2, 16))).astype(np.float32)
    sel = np.eye(2, dtype=np.float32)
    pidx = (np.arange(128) // 64) * 64 + ((np.arange(128) % 64) + 32) % 64
    permm = np.zeros((128, 128), np.float32)
    permm[pidx, np.arange(128)] = 1.0

    shared = {
        "w1i": f(ffn1_w_in), "w2i": f(ffn2_w_in), "w1o": f(ffn1_w_out), "w2o": f(ffn2_w_out),
        "pw": f(pool_w)[0], "wqkv": wqkv_ext, "wop": wop, "vecs": vecs, "ident": ident, "masks": masks,
        "maskc": maskc, "cosT": cosT, "sinT": sinT, "invc": invc, "bvb": bvb, "sinks": sinks, "sel": sel, "permm": permm,
    }
    in_maps = []
    for c in range(8):
        m = dict(shared)
        m["xp"] = x_prompt[c]
        m["xs"] = np.ascontiguousarray(x_sample[16 * c:16 * c + 16].reshape(128, D))
        m["spool"] = np.ascontiguousarray(state_pool[0, 16 * c:16 * c + 16])
        m["ck"] = np.ascontiguousarray(cache_k[0, 16 * c:16 * c + 16].reshape(16, 128, 256))
        m["cv"] = np.ascontiguousarray(cache_v[0, 16 * c:16 * c + 16].reshape(16, 128, 256))
        in_maps.append(m)

    if "nc" not in _NC_CACHE:
        _NC_CACHE["nc"] = build_program()
    nc = _NC_CACHE["nc"]
    res = run_bass_kernel_spmd(nc, in_maps, core_ids=list(range(8)))
    r = res.results
    y_prompt = np.stack([r[c]["yp"] for c in range(8)], 0)
    y_sample = np.concatenate([r[c]["ys"].reshape(16, 8, D) for c in range(8)], 0)
    new_pool_prompt = np.stack([r[c]["npp"] for c in range(8)], 0)[None]
    new_pool_sample = np.concatenate([r[c]["nps"] for c in range(8)], 0)[None]
    new_k_prompt = np.stack([r[c]["nkp"].reshape(128, 4, 64) for c in range(8)], 0)[None]
    new_v_prompt = np.stack([r[c]["nvp"].reshape(128, 4, 64) for c in range(8)], 0)[None]
    new_k_sample = np.concatenate([r[c]["nks"].reshape(16, 128, 4, 64) for c in range(8)], 0)[None]
    new_v_sample = np.concatenate([r[c]["nvs"].reshape(16, 128, 4, 64) for c in range(8)], 0)[None]
    return (y_prompt.astype(np.float32), y_sample.astype(np.float32), new_pool_prompt.astype(np.float32),
            new_pool_sample.astype(np.float32), new_k_prompt.astype(np.float32), new_v_prompt.astype(np.float32),
            new_k_sample.astype(np.float32), new_v_sample.astype(np.float32))
```

```python
import numpy as np
from contextlib import ExitStack
import concourse.bass as bass
import concourse.mybir as mybir
from concourse.bass_utils import run_bass_kernel_spmd

F32 = mybir.dt.float32
BF16 = mybir.dt.bfloat16
AF = mybir.ActivationFunctionType
ALU = mybir.AluOpType

COMPUTE = ("pe", "act", "dve", "pool")


class Res:
    __slots__ = ("name", "last_w", "readers")

    def __init__(self, name=""):
        self.name = name
        self.last_w = None
        self.readers = []


class DSem:
    __slots__ = ("sem", "count")

    def __init__(self, sem):
        self.sem = sem
        self.count = 0


class Op:
    __slots__ = ("eng", "fn", "deps", "needs_inc", "ms", "dsem", "dval", "idx")


class Prog:
    def __init__(self, nc, stack):
        self.nc = nc
        self.stack = stack
        self.ops = []
        self.esem = {}
        for e in COMPUTE:
            self.esem[e] = stack.enter_context(nc.semaphore("S_" + e))
        self.n_dsem = 0

    def dsem(self):
        self.n_dsem += 1
        return DSem(self.stack.enter_context(self.nc.semaphore("D%d" % self.n_dsem)))

    def alias(self, new_list, old_list):
        extra = []
        for old in old_list:
            extra.extend(old.readers)
            if old.last_w is not None:
                extra.append(old.last_w)
        for n in new_list:
            n.readers.extend(extra)

    def op(self, eng, fn, reads=(), writes=(), dsem=None):
        o = Op()
        o.eng = eng
        o.fn = fn
        o.needs_inc = False
        o.ms = None
        o.dsem = dsem
        o.dval = None
        o.idx = len(self.ops)
        if dsem is not None:
            dsem.count += 1
            o.dval = 16 * dsem.count
        deps = {}
        for r in reads:
            w = r.last_w
            if w is not None:
                deps[w.idx] = w
        for r in writes:
            w = r.last_w
            if w is not None:
                deps[w.idx] = w
            for rd in r.readers:
                deps[rd.idx] = rd
        latest = {}
        for d in deps.values():
            if d.dsem is None:
                if d.eng not in latest or d.idx > latest[d.eng].idx:
                    latest[d.eng] = d
        deps = {k: d for k, d in deps.items() if d.dsem is not None or latest[d.eng] is d}
        o.deps = []
        for d in deps.values():
            if d.dsem is None and d.eng == "pe" and eng == "pe" and dsem is None:
                continue
            o.deps.append(d)
            if d.dsem is None:
                d.needs_inc = True
        for r in reads:
            r.readers.append(o)
        for r in writes:
            r.last_w = o
            r.readers = []
        self.ops.append(o)
        return o

    def emit(self, final_waits=()):
        nc = self.nc
        cnt = {e: 0 for e in COMPUTE}
        for o in self.ops:
            if o.dsem is None and o.needs_inc:
                cnt[o.eng] += 1
                o.ms = cnt[o.eng]
        by_eng = {}
        for o in self.ops:
            by_eng.setdefault(o.eng, []).append(o)
        esem = self.esem

        def run(engname, engobj):
            seen_e = {e: 0 for e in COMPUTE}
            seen_d = {}
            for o in by_eng.get(engname, []):
                need_e = {}
                need_d = {}
                for d in o.deps:
                    if d.dsem is not None:
                        k = id(d.dsem)
                        if d.dval > seen_d.get(k, 0) and d.dval > need_d.get(k, (None, 0))[1]:
                            need_d[k] = (d.dsem, d.dval)
                    else:
                        if d.ms > seen_e[d.eng] and d.ms > need_e.get(d.eng, 0):
                            need_e[d.eng] = d.ms
                for e, v in need_e.items():
                    engobj.wait_ge(esem[e], v)
                    seen_e[e] = v
                for k, (ds, v) in need_d.items():
                    engobj.wait_ge(ds.sem, v)
                    seen_d[k] = v
                ins = o.fn(engobj)
                if o.dsem is not None:
                    ins.then_inc(o.dsem.sem, 16)
                elif o.needs_inc:
                    ins.then_inc(esem[o.eng], 1)
            if engname == "sp":
                for ds in final_waits:
                    if ds.count:
                        engobj.wait_ge(ds.sem, 16 * ds.count)

        with nc.Block() as block:
            @block.sync
            def _(e):
                run("sp", e)

            @block.gpsimd
            def _(e):
                run("pool", e)

            @block.tensor
            def _(e):
                run("pe", e)

            @block.scalar
            def _(e):
                run("act", e)

            @block.vector
            def _(e):
                run("dve", e)


D = 1024
T = 2176
NTP = 2048
TBS = [(0, 512), (512, 512), (1024, 512), (1536, 512), (2048, 128)]
DFF = 2816
GROUPS = [8, 8, 6]
NV = 92
C_NF1, C_NMIX, C_NF2, C_NFIN, C_PSC, C_BO, C_BQ = 0, 16, 32, 48, 56, 64, 72
EPS = 1e-6
STAGE = 99


def build_program():
    nc = bass.Bass("TRN2", target_bir_lowering=False)

    def din(n, s):
        return nc.dram_tensor(n, s, F32, kind="ExternalInput").ap()

    def dout(n, s):
        return nc.dram_tensor(n, s, F32, kind="ExternalOutput").ap()

    xp = din("xp", [2048, D])
    xs = din("xs", [128, D])
    spool = din("spool", [16, 15, D])
    ck = din("ck", [16, 128, 256])
    cv = din("cv", [16, 128, 256])
    wi = [din("w1i", [2, D, 2 * DFF]), din("w2i", [2, D, 2 * DFF])]
    wo = [din("w1o", [2, DFF, D]), din("w2o", [2, DFF, D])]
    pw = din("pw", [4, 256, 256])
    wqkv = din("wqkv", [D, 2816])
    wop = din("wop", [D, D])
    vecs_d = din("vecs", [128, NV])
    ident_d = din("ident", [128, 128])
    masks_d = din("masks", [128, 3, 128])
    maskc_d = din("maskc", [128, 8])
    cos_d = din("cosT", [128, T])
    sin_d = din("sinT", [128, T])
    invc_d = din("invc", [128, 4, 16])
    bvb_d = din("bvb", [128, 2, 128])
    sinks_d = din("sinks", [2, 16])
    sel_d = din("sel", [2, 2])
    perm_d = din("permm", [128, 128])

    yp = dout("yp", [2048, D])
    ys = dout("ys", [128, D])
    npp = dout("npp", [15, D])
    nps = dout("nps", [16, 15, D])
    nkp = dout("nkp", [128, 256])
    nvp = dout("nvp", [128, 256])
    nks = dout("nks", [16, 128, 256])
    nvs = dout("nvs", [16, 128, 256])

    with ExitStack() as st:
        P = Prog(nc, st)

        def sb(n, s, d):
            return st.enter_context(nc.sbuf_tensor("s_" + n, s, d))

        xT = sb("xT", [128, 8, T], F32)
        xn = sb("xn", [128, 8, T], BF16)
        scr = sb("scr", [128, 8704], F32)
        wout = sb("wout", [128, 8, 1024], BF16)
        win = [sb("win%d" % i, [128, 2, 8, 256], BF16) for i in range(2)]
        kT = sb("kT", [128, T], BF16)
        Vt = sb("Vt", [128, 17, 128], BF16)
        Pb = sb("Pb", [128, 4, 512], BF16)
        T32 = sb("T32", [128, 4, 512], F32)
        rs3 = sb("rs3", [128, 3, 512], F32)
        rstd = rs3[:, 0:2, :]
        rsq = rs3[:, 2, :]
        sq = sb("sq", [128, 2, 512], BF16)
        rope = sb("rope", [128, 2, 512], F32)
        vecs = sb("vecs", [128, NV], F32)
        identf = sb("identf", [128, 128], F32)
        identb = sb("identb", [128, 128], BF16)
        onesb = sb("onesb", [128, 128], BF16)
        skz = sb("skz", [128, 16], BF16)
        permb = sb("permb", [128, 128], BF16)
        masks = sb("masks", [128, 3, 128], BF16)
        maskc = sb("maskc", [128, 8], F32)
        invc = sb("invc", [128, 4, 16], F32)
        bvb = sb("bvb", [128, 2, 128], F32)
        cst = sb("cst", [128, 4], F32)
        sk16 = sb("sk16", [2, 16], F32)
        sklo = sb("sklo", [2, 16], F32)
        skhi = sb("skhi", [2, 16], BF16)
        skhl = sb("skhl", [2, 16], BF16)
        sel = sb("sel", [2, 2], F32)
        k32 = sb("k32", [128, 256], F32)
        vout = sb("vout", [128, 2, 128], F32)
        tmp16 = sb("tmp16", [128, 16], F32)
        psall = st.enter_context(nc.psum_tensor("psall", [128, 4096], F32))
        PS = [psall[:, i * 512:(i + 1) * 512] for i in range(8)]


        def MM(out, lhsT, rhs, start, stop, reads, writes):
            P.op("pe", lambda e: e.matmul(out, lhsT, rhs, start=start, stop=stop), reads, writes)

        R_cf = Res()

        def TR(out, in_, idn, reads, writes):
            P.op("pe", lambda e: e.transpose(out=out, in_=in_, identity=idn), list(reads) + [R_cf], writes)

        def ACT(out, in_, func, reads, writes, **kw):
            P.op("act", lambda e: e.activation(out=out, in_=in_, func=func, **kw), reads, writes)

        def TT(eng, out, in0, in1, op, reads, writes):
            P.op(eng, lambda e: e.tensor_tensor(out=out, in0=in0, in1=in1, op=op), reads, writes)

        def STT(out, in0, scalar, in1, op0, op1, reads, writes):
            P.op("dve", lambda e: e.scalar_tensor_tensor(out=out, in0=in0, scalar=scalar, in1=in1, op0=op0, op1=op1), reads, writes)

        def DMA(eng, out, in_, reads, writes, dsem):
            P.op(eng, lambda e: e.dma_start(out=out, in_=in_), reads, writes, dsem=dsem)

        def RECIP(out, in_, reads, writes):
            P.op("dve", lambda e: e.reciprocal(out=out, in_=in_), reads, writes)

        def VCOPY(out, in_, reads, writes):
            P.op("dve", lambda e: e.tensor_copy(out=out, in_=in_), reads, writes)

        hT = scr[:].bitcast(BF16).rearrange("p (c t) -> p c t", c=8)

        R_x = [[Res() for _ in TBS] for _ in range(8)]
        R_xn = [[Res() for _ in TBS] for _ in range(8)]
        R_h = [[Res() for _ in TBS] for _ in range(8)]
        R_ps = [Res() for _ in range(8)]
        R_win = [Res(), Res()]
        R_wout = Res()
        R_T32 = [Res() for _ in range(4)]
        R_rstd = [Res(), Res()]
        R_rsq = Res()
        R_sq = [Res(), Res()]
        R_rope = Res()
        R_const = Res()
        R_Pb = [Res() for _ in range(4)]
        R_kT = [Res() for _ in TBS]
        R_V = [Res() for _ in range(17)]
        R_k32 = Res()
        R_kout = Res()
        R_vout = Res()
        R_tmp16 = Res()
        R_sink = Res()
        allh = [r for row in R_h for r in row]
        allxn = [r for row in R_xn for r in row]

        D_win = [P.dsem(), P.dsem()]
        D_wout = P.dsem()
        D_inA, D_inB, D_inS, D_inA2, D_inB2 = P.dsem(), P.dsem(), P.dsem(), P.dsem(), P.dsem()
        D_const = P.dsem()
        D_cf = P.dsem()
        D_constP = P.dsem()
        D_rope = P.dsem()
        D_stg = P.dsem()
        D_kstg = P.dsem()
        D_vc = P.dsem()
        D_out = [P.dsem(), P.dsem()]
        D_o2 = P.dsem()
        D_ko = P.dsem()
        D_vo = P.dsem()
        D_d2d = P.dsem()

        DMA("pool", masks[:], masks_d, [], [R_const], D_constP)
        DMA("sp", identf[:], ident_d, [], [R_cf], D_cf)
        for t, d in [(vecs, vecs_d), (maskc, maskc_d),
                     (invc, invc_d), (bvb, bvb_d), (sk16, sinks_d), (sel, sel_d)]:
            DMA("sp", t[:], d, [], [R_const], D_const)
        DMA("pool", identb[:], ident_d, [], [R_const], D_constP)
        DMA("pool", permb[:], perm_d, [], [R_const], D_constP)
        P.op("dve", lambda e: e.memset(onesb[:], 1.0), writes=[R_const])
        P.op("dve", lambda e: e.memset(cst[:, 0:1], EPS), writes=[R_const])
        ACT(sk16[:], sk16[:], AF.Exp, [R_const], [R_sink])
        VCOPY(skhi[:], sk16[:], [R_sink], [R_sink])
        TT("dve", sklo[:], sk16[:], skhi[:], ALU.subtract, [R_sink], [R_sink])
        P.op("dve", lambda e: e.tensor_scalar(out=sklo[:], in0=sklo[:], scalar1=sel[:, 1:2], scalar2=None, op0=ALU.mult), [R_sink, R_const], [R_sink])
        STT(skhl[:], skhi[:], sel[:, 0:1], sklo[:], ALU.mult, ALU.add, [R_sink, R_const], [R_sink])
        P.op("dve", lambda e: e.memset(skz[:], 0.0), [], [R_sink])
        VCOPY(skz[0:2, :], skhl[:], [R_sink], [R_sink])

        slot_ctr = [0]

        def load_win_unit(Wi, u):
            s = slot_ctr[0] % 2
            slot_ctr[0] += 1
            for gu in range(2):
                c0 = gu * DFF + u * 256
                DMA("pool", win[s][:, gu, :, :], Wi[:, c0:c0 + 256].rearrange("(k p) n -> p k n", p=128),
                    [], [R_win[s]], D_win[s])
            return s

        def load_wout(Wo, c0, n):
            DMA("pool", wout[:, 0:n, :], Wo[c0 * 128:(c0 + n) * 128, :].rearrange("(c p) n -> p c n", p=128),
                [], [R_wout], D_wout)

        nctr = [0]

        def rstd_tb(tb):
            t0, n = TBS[tb]
            i = nctr[0] % 2
            nctr[0] += 1
            for c in range(8):
                j = c % 2
                ACT(sq[:, j, 0:n], xT[:, c, t0:t0 + n], AF.Square, [R_x[c][tb]], [R_sq[j]])
                MM(PS[6][:, 0:n], onesb[:], sq[:, j, 0:n], c == 0, c == 7, [R_sq[j], R_const], [R_ps[6]])
            ACT(rsq[:, 0:n], PS[6][:, 0:n], AF.Ln, [R_ps[6], R_const], [R_rsq], bias=cst[:, 0:1], scale=1.0 / D)
            ACT(rstd[:, i, 0:n], rsq[:, 0:n], AF.Exp, [R_rsq], [R_rstd[i]], scale=-0.5)
            return i

        def norm_to_xn(tb, gcol):
            t0, n = TBS[tb]
            i = rstd_tb(tb)
            for c in range(8):
                STT(xn[:, c, t0:t0 + n], xT[:, c, t0:t0 + n], vecs[:, gcol + c:gcol + c + 1], rstd[:, i, 0:n],
                    ALU.mult, ALU.mult, [R_x[c][tb], R_rstd[i], R_const], [R_xn[c][tb]])

        pending = []

        def drain(k=None):
            while pending and (k is None or k > 0):
                pending.pop(0)()
                if k is not None:
                    k -= 1

        def norm_pieces(tb, gcol):
            t0, n = TBS[tb]
            holder = {}

            def p_sq(c):
                def f():
                    if c == 0:
                        holder["i"] = nctr[0] % 2
                        nctr[0] += 1
                    j = c % 2
                    ACT(sq[:, j, 0:n], xT[:, c, t0:t0 + n], AF.Square, [R_x[c][tb]], [R_sq[j]])
                    MM(PS[6][:, 0:n], onesb[:], sq[:, j, 0:n], c == 0, c == 7, [R_sq[j], R_const], [R_ps[6]])
                return f

            def p_rs():
                i = holder["i"]
                ACT(rsq[:, 0:n], PS[6][:, 0:n], AF.Ln, [R_ps[6], R_const], [R_rsq], bias=cst[:, 0:1], scale=1.0 / D)
                ACT(rstd[:, i, 0:n], rsq[:, 0:n], AF.Exp, [R_rsq], [R_rstd[i]], scale=-0.5)

            def p_st(c):
                def f():
                    i = holder["i"]
                    STT(xn[:, c, t0:t0 + n], xT[:, c, t0:t0 + n], vecs[:, gcol + c:gcol + c + 1], rstd[:, i, 0:n],
                        ALU.mult, ALU.mult, [R_x[c][tb], R_rstd[i], R_const], [R_xn[c][tb]])
                return f

            return [p_sq(c) for c in range(8)] + [p_rs] + [p_st(c) for c in range(8)]

        stA = scr[:, 0:8192].rearrange("p (i d) -> p i d", i=8)
        stB = xn[:].bitcast(F32).rearrange("p c t -> p (c t)")[:, 0:8192].rearrange("p (i d) -> p i d", i=8)
        stS = wout[:].bitcast(F32).rearrange("p c t -> p (c t)")[:, 0:1024]
        R_st = [Res() for _ in range(5)]
        for hh, (stv, dd) in enumerate([(stA, D_inA), (stA, D_inA2), (stB, D_inB), (stB, D_inB2)]):
            lo = (hh % 2) * 4
            DMA("sp", stv[:, lo:lo + 4, :], xp[hh * 512:(hh + 1) * 512, :].rearrange("(i p) d -> p i d", p=128), [], [R_st[hh]], dd)
        DMA("sp", stS, xs, [], [R_st[4]], D_inS)
        tctr = 0
        for tb in range(5):
            t0, n = TBS[tb]
            for c in range(8):
                b = 4 + (tctr % 4)
                for tt in range(n // 128):
                    ti = t0 // 128 + tt
                    if ti < 8:
                        src, rs = stA[:, ti, c * 128:(c + 1) * 128], R_st[ti // 4]
                    elif ti < 16:
                        src, rs = stB[:, ti - 8, c * 128:(c + 1) * 128], R_st[ti // 4]
                    else:
                        src, rs = stS[:, c * 128:(c + 1) * 128], R_st[4]
                    TR(PS[b][:, tt * 128:(tt + 1) * 128], src, identf[:], [rs], [R_ps[b]])
                if tctr % 2 == 0:
                    ACT(xT[:, c, t0:t0 + n], PS[b][:, 0:n], AF.Copy, [R_ps[b]], [R_x[c][tb]])
                else:
                    VCOPY(xT[:, c, t0:t0 + n], PS[b][:, 0:n], [R_ps[b]], [R_x[c][tb]])
                tctr += 1
        P.alias(allh, [R_st[0], R_st[1]])
        P.alias(allxn, [R_st[2], R_st[3]])
        P.alias([R_wout], [R_st[4]])
        DMA("sp", nks[:, 0:120, :], ck[:, 8:128, :], [], [], D_d2d)
        DMA("sp", nvs[:, 0:120, :], cv[:, 8:128, :], [], [], D_d2d)
        DMA("sp", nps[:, 0:7, :], spool[:, 8:15, :], [], [], D_d2d)

        actr = [0]
        bctr = [0]

        def ffn(l, f, gcol, prefetched, do_norm, post_tb, mid_hook=None):
            Wi = wi[f][l]
            Wo = wo[f][l]
            if do_norm:
                for tb in range(5):
                    norm_to_xn(tb, gcol)
            nunits = DFF // 256
            loaded = dict(prefetched)
            nxt = [max(loaded.keys()) + 1 if loaded else 0]

            def ensure(u):
                while nxt[0] <= u and nxt[0] < nunits:
                    loaded[nxt[0]] = load_win_unit(Wi, nxt[0])
                    nxt[0] += 1

            c0 = 0
            for gi, ng in enumerate(GROUPS):
                last = (gi == len(GROUPS) - 1)
                if gi == 0:
                    ensure(1)
                    load_wout(Wo, 0, ng)
                for u in range(c0 // 2, (c0 + ng) // 2):
                    ensure(u)
                    s = loaded[u]
                    for half in range(2):
                        cl = 2 * u + half - c0
                        for tb in range(5):
                            t0, n = TBS[tb]
                            ab = actr[0] % 2
                            actr[0] += 1
                            pg, pu = 2 * ab, 2 * ab + 1
                            for gu, pb in ((0, pg), (1, pu)):
                                for k in range(8):
                                    MM(PS[pb][:, 0:n], win[s][:, gu, k, half * 128:(half + 1) * 128], xn[:, k, t0:t0 + n],
                                       k == 0, k == 7, [R_win[s], R_xn[k][tb]], [R_ps[pb]])
                            ACT(T32[:, ab, 0:n], PS[pg][:, 0:n], AF.Silu, [R_ps[pg]], [R_T32[ab]])
                            TT("dve", hT[:, cl, t0:t0 + n], PS[pu][:, 0:n], T32[:, ab, 0:n], ALU.mult,
                               [R_ps[pu], R_T32[ab]], [R_h[cl][tb]])
                    ensure(u + 1)
                if last and mid_hook is not None:
                    mid_hook()
                for tb in range(5):
                    t0, n = TBS[tb]
                    for m in range(8):
                        pb = (4, 5, 7)[bctr[0] % 3]
                        bctr[0] += 1
                        for cl in range(ng):
                            MM(PS[pb][:, 0:n], wout[:, cl, m * 128:(m + 1) * 128], hT[:, cl, t0:t0 + n],
                               cl == 0, cl == ng - 1, [R_wout, R_h[cl][tb]], [R_ps[pb]])
                        STT(xT[:, m, t0:t0 + n], PS[pb][:, 0:n], 0.5, xT[:, m, t0:t0 + n], ALU.mult, ALU.add,
                            [R_ps[pb], R_x[m][tb]], [R_x[m][tb]])
                        drain(4)
                    if last and post_tb is not None:
                        pending.extend(post_tb(tb))
                if last:
                    drain()
                c0 += ng
                if gi + 1 < len(GROUPS):
                    load_wout(Wo, c0, GROUPS[gi + 1])

        def prefetch_first(l, f):
            Wi = wi[f][l]
            return {0: load_win_unit(Wi, 0), 1: load_win_unit(Wi, 1)}

        def pool_mixer(post_tb):
            gcol = C_NMIX + 0
            E = [scr[:, c * 528:c * 528 + 527] for c in range(8)]
            AB = [[scr[:, 4224 + (2 * s + q) * 528:4224 + (2 * s + q) * 528 + 527] for q in range(2)] for s in range(2)]
            stg = scr[:, 6336:6336 + 2048].rearrange("p (h d) -> p h d", h=2)
            R_E = [Res() for _ in range(8)]
            R_AB = [[Res(), Res()], [Res(), Res()]]
            R_stg = Res()
            newres = R_E + [R_AB[0][0], R_AB[0][1], R_AB[1][0], R_AB[1][1], R_stg]
            P.alias(newres, allh)
            pwt = wout[:].rearrange("p c t -> p (c t)")[:, 0:2048].rearrange("p (g i n) -> p g i n", g=4, i=2)
            DMA("pool", pwt, pw.rearrange("g (i p) n -> p g i n", p=128), [], [R_wout], D_wout)
            for h in range(2):
                DMA("sp", stg[0:120, h, :], spool[8 * h:8 * h + 8, :, :], [], [R_stg], D_stg)
            ustage = T32[:, 2:4, :].rearrange("p a b -> p (a b)")
            R_us = [R_T32[2], R_T32[3]]

            def out_transposes(srcs, out_ap, in_ap):
                for cb in range(2):
                    for cc in range(4):
                        c = cb * 4 + cc
                        TR(PS[7][:, cc * 128:(cc + 1) * 128], srcs[c], identf[:], [R_E[c], R_const], [R_ps[7]])
                    ACT(ustage[:, cb * 512:(cb + 1) * 512], PS[7][:, :], AF.Copy, [R_ps[7]], [R_us[cb]])
                DMA("sp", out_ap, in_ap, R_us, [], D_o2)

            prb = [(T32[:, 0, :], R_T32[0]), (T32[:, 1, :], R_T32[1]), (T32[:, 0, :], R_T32[0]), (T32[:, 1, :], R_T32[1]),
                   (rope[:, 0, :], R_rope)]

            def rstd_into(tb):
                t0, n = TBS[tb]
                buf, rbuf = prb[tb]
                for c in range(8):
                    j = c % 2
                    ACT(sq[:, j, 0:n], xT[:, c, t0:t0 + n], AF.Square, [R_x[c][tb]], [R_sq[j]])
                    MM(PS[6][:, 0:n], onesb[:], sq[:, j, 0:n], c == 0, c == 7, [R_sq[j], R_const], [R_ps[6]])
                ACT(rsq[:, 0:n], PS[6][:, 0:n], AF.Ln, [R_ps[6], R_const], [R_rsq], bias=cst[:, 0:1], scale=1.0 / D)
                ACT(buf[:, 0:n], rsq[:, 0:n], AF.Exp, [R_rsq], [rbuf], scale=-0.5)

            state = {}

            def geo(tb):
                if tb < 4:
                    return 527, (lambda ap, a, b: ap[:, a:b]), (lambda ap: ap)
                return 23, (lambda ap, a, b: ap[:, :, a:b]), (lambda ap: ap.rearrange("p (s t) -> p s t", s=16))

            def U(tb, c):
                t0, n = TBS[tb]
                L, sl, vw = geo(tb)
                buf, rbuf = prb[tb]
                if tb < 4:
                    Ec = E[c]
                    if tb == 0:
                        P.op("pool", lambda e, Ec=Ec: e.memset(Ec[:, 0:15], 0.0), writes=[R_E[c]])
                    else:
                        ACT(Ec[:, 0:15], Ec[:, 512:527], AF.Copy, [R_E[c]], [R_E[c]])
                else:
                    Ec = vw(E[c][:, 0:368])
                    for h in range(2):
                        TR(PS[7][:, h * 120:(h + 1) * 120], stg[0:120, h, c * 128:(c + 1) * 128], identf[0:120, 0:120],
                           [R_stg, R_const], [R_ps[7]])
                    ACT(Ec[:, :, 0:15], PS[7][:, 0:240].rearrange("p (s t) -> p s t", s=16), AF.Copy, [R_ps[7]], [R_E[c]])
                STT(sl(Ec, 15, L), vw(xT[:, c, t0:t0 + n]), vecs[:, gcol + c:gcol + c + 1], vw(buf[:, 0:n]), ALU.mult, ALU.mult,
                    [R_x[c][tb], rbuf, R_const], [R_E[c]])
                state[(tb, c)] = Ec

            def A(tb, c):
                L, sl, vw = geo(tb)
                g = c // 2
                s = c % 2
                eng = "dve" if c >= 6 else "pool"
                src, rsrc = state[(tb, c)], R_E[c]
                for k in range(g + 1):
                    d = 2 ** k
                    lo = 2 * d - 1
                    dst = AB[s][k % 2] if tb < 4 else vw(AB[s][k % 2][:, 0:368])
                    rdst = R_AB[s][k % 2]
                    TT(eng, sl(dst, lo, L), sl(src, lo, L), sl(src, lo - d, L - d), ALU.add, [rsrc], [rdst])
                    src, rsrc = dst, rdst
                state[("s", tb, c)] = (src, rsrc)

            def Pp(tb, c):
                t0, n = TBS[tb]
                L, sl, vw = geo(tb)
                g = c // 2
                w = 2 ** (g + 1)
                src, rsrc = state[("s", tb, c)]
                Ec = state[(tb, c)]
                STT(vw(xn[:, c, t0:t0 + n]), sl(src, 15, L), 1.0 / w, sl(Ec, 15, L), ALU.mult, ALU.subtract,
                    [rsrc, R_E[c]], [R_xn[c][tb]])
                if tb == 0:
                    TT("dve", tmp16[:], src[:, 15:31], invc[:, g, :], ALU.mult, [rsrc, R_const], [R_tmp16])
                    TT("dve", xn[:, c, 0:16], tmp16[:], Ec[:, 15:31], ALU.subtract, [R_tmp16, R_E[c]], [R_xn[c][tb]])

            def linmap(tb):
                t0, n = TBS[tb]
                for oc in range(8):
                    g = oc // 2
                    pb = 4 + (bctr[0] % 2)
                    bctr[0] += 1
                    for icl in range(2):
                        MM(PS[pb][:, 0:n], pwt[:, g, icl, (oc % 2) * 128:(oc % 2) * 128 + 128], xn[:, 2 * g + icl, t0:t0 + n],
                           icl == 0, icl == 1, [R_wout, R_xn[2 * g + icl][tb]], [R_ps[pb]])
                    STT(xT[:, oc, t0:t0 + n], PS[pb][:, 0:n], vecs[:, C_PSC + oc:C_PSC + oc + 1], xT[:, oc, t0:t0 + n],
                        ALU.mult, ALU.add, [R_ps[pb], R_x[oc][tb], R_const], [R_x[oc][tb]])
                    drain(3)
                pending.extend(post_tb(tb))

            rstd_into(0)
            seq = [(tb, c) for tb in range(5) for c in range(8)]
            for k, (tb, c) in enumerate(seq):
                if k >= 2:
                    Pp(*seq[k - 2])
                    if seq[k - 2][1] == 7:
                        linmap(seq[k - 2][0])
                if tb == 4 and c == 0:
                    out_transposes([E[cc][:, 399:527] for cc in range(8)], npp, ustage[113:128, :])
                U(tb, c)
                A(tb, c)
                if c == 5 and tb < 4:
                    rstd_into(tb + 1)
            Pp(*seq[-2])
            Pp(*seq[-1])
            linmap(4)
            ssrc = []
            for c in range(8):
                dstc = E[c][:, 384:512]
                P.op("pool", lambda e, dstc=dstc, c=c: e.tensor_copy(
                    out=dstc.rearrange("p (s t) -> p s t", s=16),
                    in_=E[c][:, 0:368].rearrange("p (s t) -> p s t", s=16)[:, :, 15:23]), [R_E[c]], [R_E[c]])
                ssrc.append(dstc)
            out_transposes(ssrc, nps[:, 7:15, :], ustage[:, :])
            drain()
            P.alias(allh, newres)

        def attn_early_loads(j, what="wk"):
            base = j * 1408
            kstg_ = Pb[:].rearrange("p a b -> p (a b)").rearrange("p (s k) -> p s k", s=16)
            if "w" in what:
                for s in range(2):
                    for gu in range(2):
                        c0 = base + s * 512 + gu * 256
                        DMA("pool", win[s][:, gu, :, :], wqkv[:, c0:c0 + 256].rearrange("(k p) n -> p k n", p=128),
                            [], [R_win[s]], D_win[s])
            if "k" in what:
                DMA("pool", kstg_, ck[:, :, j * 128:(j + 1) * 128].rearrange("s k c -> k s c"), [], R_Pb, D_kstg)

        def attn_mixer(post_tb):
            wreg = wout[:].rearrange("p c t -> p (c t)")
            kvw = wreg[:, 0:3072].rearrange("p (k n) -> p k n", k=8)
            KcT = wreg[:, 3072:5120].rearrange("p (s k) -> p s k", s=16)
            Vc = wreg[:, 5120:7168].rearrange("p (s k) -> p s k", s=16)
            kstg = Pb[:].rearrange("p a b -> p (a b)").rearrange("p (s k) -> p s k", s=16)
            R_kvw, R_KcT, R_Vc = Res(), Res(), Res()
            P.alias([R_kvw, R_KcT, R_Vc], [R_wout])
            R_q = R_h
            kT_lo = kT
            kT_hi = rs3[:].rearrange("p a b -> p (a b)").bitcast(BF16)[:, 0:T]
            P.alias(R_kT, R_rstd + [R_rsq])
            P.op("pool", lambda e: e.memset(kT_lo[64:128, :], 0.0), [], R_kT)
            P.op("pool", lambda e: e.memset(kT_hi[0:64, :], 0.0), [], R_kT)
            exb = rope[:].rearrange("p a b -> p (a b)").bitcast(BF16).rearrange("p (a b) -> p a b", a=4)
            R_ex = [Res() for _ in range(4)]
            sctr = 0
            dma_q = []
            psb = PS[7][:].bitcast(BF16)
            for j in range(2):
                base = j * 1408
                if j == 0:
                    DMA("pool", kvw, wqkv[:, base + 1024:base + 1408].rearrange("(k p) n -> p k n", p=128), [], [R_kvw], D_wout)
                else:
                    attn_early_loads(1, "k")
                DMA("pool", Vc, cv[:, :, j * 128:(j + 1) * 128].rearrange("s k c -> k s c"), [], [R_Vc], D_vc)
                qfin = []
                for tb in range(5):
                    t0, n = TBS[tb]
                    if tb == 0:
                        P.alias([R_rope], R_ex)
                    DMA("sp", rope[:, 0, 0:n], cos_d[:, t0:t0 + n], [], [R_rope], D_rope)
                    DMA("sp", rope[:, 1, 0:n], sin_d[:, t0:t0 + n], [], [R_rope], D_rope)
                    for ch in range(5):
                        ab = actr[0] % 2
                        actr[0] += 1
                        p1, p2 = 2 * ab, 2 * ab + 1
                        bcol = C_BQ + j * 10 + 2 * ch
                        rt = [R_T32[p1], R_T32[p2]]
                        if ch < 4:
                            s_, gu = ch // 2, ch % 2
                            for k in range(8):
                                MM(PS[p1][:, 0:n], win[s_][:, gu, k, 0:128], xn[:, k, t0:t0 + n], k == 0, k == 7,
                                   [R_win[s_], R_xn[k][tb]], [R_ps[p1]])
                            P.op("dve", lambda e, o_=sq[:, ab, 0:n], i_=PS[p1][:, 0:n], b_=vecs[:, bcol:bcol + 1]: e.tensor_scalar(
                                out=o_, in0=i_, scalar1=b_, scalar2=None, op0=ALU.add), [R_ps[p1], R_const], [R_sq[ab]])

                            def fin(p1=p1, p2=p2, ab=ab, bcol=bcol, ch=ch, t0=t0, n=n, tb=tb, rt=rt):
                                MM(PS[p2][:, 0:n], permb[:], sq[:, ab, 0:n], True, True, [R_sq[ab], R_const], [R_ps[p2]])
                                STT(T32[:, p1, 0:n], PS[p1][:, 0:n], vecs[:, bcol:bcol + 1], rope[:, 0, 0:n], ALU.add, ALU.mult,
                                    [R_ps[p1], R_rope, R_const], [R_T32[p1]])
                                TT("dve", T32[:, p2, 0:n], PS[p2][:, 0:n], rope[:, 1, 0:n], ALU.mult, [R_ps[p2], R_rope], [R_T32[p2]])
                                TT("pool", hT[:, ch, t0:t0 + n], T32[:, p1, 0:n], T32[:, p2, 0:n], ALU.add, rt, [R_q[ch][tb]])

                            if qfin:
                                qfin.pop(0)()
                            qfin.append(fin)
                        else:
                            while qfin:
                                qfin.pop(0)()
                            for which, pp in ((0, p1), (1, p2)):
                                for k in range(8):
                                    MM(PS[pp][:, 0:n], kvw[:, k, which * 128:(which + 1) * 128], xn[:, k, t0:t0 + n], k == 0, k == 7,
                                       [R_kvw, R_xn[k][tb]], [R_ps[pp]])
                            STT(T32[:, p1, 0:n], PS[p1][:, 0:n], vecs[:, bcol:bcol + 1], rope[:, 0, 0:n], ALU.add, ALU.mult,
                                [R_ps[p1], R_rope, R_const], [R_T32[p1]])
                            STT(T32[:, p2, 0:n], PS[p2][:, 0:n], vecs[:, bcol + 1:bcol + 2], rope[:, 1, 0:n], ALU.add, ALU.mult,
                                [R_ps[p2], R_rope, R_const], [R_T32[p2]])
                            TT("pool", kT_lo[0:64, t0:t0 + n], T32[0:64, p1, 0:n], T32[0:64, p2, 0:n], ALU.add, rt, [R_kT[tb]])
                            TT("pool", kT_hi[64:128, t0:t0 + n], T32[64:128, p1, 0:n], T32[64:128, p2, 0:n], ALU.add, rt, [R_kT[tb]])
                            if tb == 3:
                                TT("pool", k32[:, 0:128], T32[:, p1, 384:512], T32[:, p2, 384:512], ALU.add, rt, [R_k32])
                            if tb == 4:
                                TT("pool", k32[:, 128:256], T32[:, p1, 0:128], T32[:, p2, 0:128], ALU.add, rt, [R_k32])
                    for tt in range(n // 128):
                        ti = t0 // 128 + tt
                        pb = 4 + (bctr[0] % 2)
                        bctr[0] += 1
                        for k in range(8):
                            MM(PS[pb][:, 0:128], xn[:, k, ti * 128:(ti + 1) * 128], kvw[:, k, 256:384], k == 0, k == 7,
                               [R_kvw, R_xn[k][tb]], [R_ps[pb]])
                        TT("dve", Vt[:, ti, :], PS[pb][:, 0:128], bvb[:, j, :], ALU.add, [R_ps[pb], R_const], [R_V[ti]])
                        if ti >= 15:
                            TT("dve", vout[:, ti - 15, :], PS[pb][:, 0:128], bvb[:, j, :], ALU.add, [R_ps[pb], R_const], [R_vout])

                for half in range(2):
                    for ss in range(8):
                        TR(psb[:, ss * 128:(ss + 1) * 128], kstg[:, half * 8 + ss, :], identb[:], R_Pb + [R_const], [R_ps[7]])
                    ACT(KcT[:, half * 8:(half + 1) * 8, :], psb.rearrange("p (s k) -> p s k", s=8), AF.Copy, [R_ps[7]], [R_KcT])
                if j == 1:
                    pf_holder["pf"] = {}
                    Wi_n = wi[1][1]
                    for u_ in range(2):
                        dma_q.append(lambda u_=u_: pf_holder["pf"].__setitem__(u_, load_win_unit(Wi_n, u_)))
                if j == 0:
                    def ld_w(s_, gu_):
                        c0_ = 1408 + s_ * 512 + gu_ * 256
                        DMA("pool", win[s_][:, gu_, :, :], wqkv[:, c0_:c0_ + 256].rearrange("(k p) n -> p k n", p=128),
                            [], [R_win[s_]], D_win[s_])
                    for s_ in range(2):
                        for gu_ in range(2):
                            dma_q.append(lambda s_=s_, gu_=gu_: ld_w(s_, gu_))
                    dma_q.append(lambda: DMA("pool", kvw, wqkv[:, 1408 + 1024:1408 + 1408].rearrange("(k p) n -> p k n", p=128),
                                             [], [R_kvw], D_wout))
                for t in range(2):
                    TR(PS[7][:, t * 128:(t + 1) * 128], k32[:, t * 128:(t + 1) * 128], identf[:], [R_k32, R_const], [R_ps[7]])
                ACT(rsq[:, 256:512], PS[7][:, 0:256], AF.Copy, [R_ps[7]], [R_rsq])
                DMA("sp", nkp[:, j * 128:(j + 1) * 128], rsq[:, 256:384], [R_rsq], [], D_ko)
                DMA("sp", nks[:, 120:128, j * 128:(j + 1) * 128], rsq[:, 384:512], [R_rsq], [], D_ko)
                DMA("sp", nvp[:, j * 128:(j + 1) * 128], vout[:, 0, :], [R_vout], [], D_vo)
                DMA("sp", nvs[:, 120:128, j * 128:(j + 1) * 128], vout[:, 1, :], [R_vout], [], D_vo)
                v4 = lambda ap: ap.rearrange("p (h q) -> p h q", h=4)
                if j == 0:
                    blocks = [(gg, b) for gg in range(2) for b in range(17)]
                else:
                    blocks = [(0, 16), (1, 16)] + [(gg, b) for gg in range(2) for b in range(16)]
                P.alias(R_ex, [R_rope])

                def stageA(gg, b, ab, cur_eng="pool"):
                    rows = slice(gg * 64, gg * 64 + 64)
                    tb = min(b // 4, 4)
                    blk0 = b * 128
                    s0, s1 = 2 * ab, 2 * ab + 1
                    rq = [R_q[ch][tb] for ch in range(4)]
                    qv = hT[:, 0:4, blk0:blk0 + 128]
                    kTg = kT_lo if gg == 0 else kT_hi
                    MM(v4(PS[s0][:, :]), kTg[:, blk0:blk0 + 128], qv, True, True, rq + [R_kT[tb]], [R_ps[s0]])
                    have_prev = (0 < b < 16)
                    if have_prev:
                        MM(v4(PS[s1][:, :]), kTg[:, blk0 - 128:blk0], qv, True, True, rq + [R_kT[(b - 1) // 4]], [R_ps[s1]])
                    if b == 16:
                        for s in range(16):
                            MM(v4(PS[s1][:, s * 32:(s + 1) * 32]), KcT[rows, s, :], hT[rows, 0:4, 2048 + s * 8:2048 + s * 8 + 8],
                               True, True, rq + [R_KcT], [R_ps[s1]])
                    mi = 0 if b < 16 else 2
                    if have_prev or b == 16:
                        ACT(exb[:, s0:s0 + 2, :].rearrange("p a b -> p (a b)"), psall[:, s0 * 512:(s0 + 2) * 512], AF.Exp,
                            [R_ps[s0], R_ps[s1]], [R_ex[s0], R_ex[s1]], scale=0.125)
                    else:
                        ACT(exb[:, s0, :], PS[s0][:, :], AF.Exp, [R_ps[s0]], [R_ex[s0]], scale=0.125)
                    TT(cur_eng, v4(Pb[:, s0, :]), v4(exb[:, s0, :]), masks[:, mi, :].unsqueeze(1).broadcast_to([128, 4, 128]),
                       ALU.mult, [R_ex[s0], R_const], [R_Pb[s0]])
                    if have_prev or b == 16:
                        if b < 16:
                            TT("dve", v4(Pb[:, s1, :]), v4(exb[:, s1, :]), masks[:, 1, :].unsqueeze(1).broadcast_to([128, 4, 128]),
                               ALU.mult, [R_ex[s1], R_const], [R_Pb[s1]])
                        else:
                            TT("dve", Pb[:, s1, :].rearrange("p (a t) -> p a t", t=8), exb[:, s1, :].rearrange("p (a t) -> p a t", t=8),
                               maskc[:].unsqueeze(1).broadcast_to([128, 64, 8]), ALU.mult, [R_ex[s1], R_const], [R_Pb[s1]])

                def stageB(gg, b, ab):
                    g = 2 * j + gg
                    rows = slice(gg * 64, gg * 64 + 64)
                    tb = min(b // 4, 4)
                    blk0 = b * 128
                    s0, s1 = 2 * ab, 2 * ab + 1
                    po = 4 + ab
                    pd = 6 + ab
                    have_prev = (0 < b < 16)
                    two = have_prev or b == 16
                    MM(PS[po][:, :], Vt[:, b, :], Pb[:, s0, :], True, not two, [R_V[b], R_Pb[s0]], [R_ps[po]])
                    if have_prev:
                        MM(PS[po][:, :], Vt[:, b - 1, :], Pb[:, s1, :], False, True, [R_V[b - 1], R_Pb[s1]], [R_ps[po]])
                    if b == 16:
                        for s in range(16):
                            MM(v4(PS[po][:, :])[:, :, s * 8:(s + 1) * 8], Vc[:, s, :], v4(Pb[:, s1, s * 32:(s + 1) * 32]),
                               False, s == 15, [R_Vc, R_Pb[s1]], [R_ps[po]])
                    MM(v4(PS[pd][:, :]), onesb[:], skz[:, 4 * g:4 * g + 4].unsqueeze(2).broadcast_to([128, 4, 128]),
                       True, False, [R_sink, R_const], [R_ps[pd]])
                    MM(PS[pd][:, :], onesb[:], Pb[:, s0, :], False, not two, [R_Pb[s0], R_const], [R_ps[pd]])
                    if have_prev:
                        MM(PS[pd][:, :], onesb[:], Pb[:, s1, :], False, True, [R_Pb[s1], R_const], [R_ps[pd]])
                    if b == 16:
                        MM(PS[pd][:, :].rearrange("p (h s t) -> p h s t", h=4, s=16), onesb[:],
                           Pb[:, s1, :].rearrange("p (s h t) -> p h s t", s=16, h=4), False, True, [R_Pb[s1], R_const], [R_ps[pd]])
                    ACT(T32[rows, ab, :], PS[pd][rows, :], AF.Ln, [R_ps[pd]], [R_T32[ab]])
                    ACT(T32[rows, ab, :], T32[rows, ab, :], AF.Exp, [R_T32[ab]], [R_T32[ab]], scale=-1.0)
                    if j == 0:
                        odst, ores = hT[rows, 4:8, blk0:blk0 + 128], [R_h[4 + hh][tb] for hh in range(4)]
                    else:
                        odst, ores = xn[rows, 4:8, blk0:blk0 + 128], [R_xn[4 + hh][tb] for hh in range(4)]
                    TT("dve", odst, v4(PS[po][rows, :]), v4(T32[rows, ab, :]), ALU.mult, [R_ps[po], R_T32[ab]], ores)

                stageA(blocks[0][0], blocks[0][1], sctr % 2)
                dve_left = 0
                for bi in range(len(blocks)):
                    if dma_q and bi >= 4 and bi % 3 == 1:
                        dma_q.pop(0)()
                        dve_left = 2
                    if bi + 1 < len(blocks):
                        stageA(blocks[bi + 1][0], blocks[bi + 1][1], (sctr + 1) % 2, "dve" if dve_left > 0 else "pool")
                        dve_left -= 1
                    stageB(blocks[bi][0], blocks[bi][1], sctr % 2)
                    sctr += 1
                    if j == 1 and bi == 1:
                        dve_left = 2
                        P.alias([R_wout], [R_kvw, R_KcT, R_Vc])
                        DMA("pool", wout[:], wop.rearrange("(c p) n -> p c n", p=128), [], [R_wout], D_wout)
            while dma_q:
                dma_q.pop(0)()
            P.alias(R_rstd + [R_rsq], R_kT)
            for tb in range(5):
                t0, n = TBS[tb]
                for m in range(8):
                    pb = (4, 5, 7)[bctr[0] % 3]
                    bctr[0] += 1
                    for c in range(8):
                        if c < 4:
                            src, rs = hT[:, 4 + c, t0:t0 + n], R_h[4 + c][tb]
                        else:
                            src, rs = xn[:, c, t0:t0 + n], R_xn[c][tb]
                        MM(PS[pb][:, 0:n], wout[:, c, m * 128:(m + 1) * 128], src, c == 0, c == 7, [R_wout, rs], [R_ps[pb]])
                    STT(xT[:, m, t0:t0 + n], PS[pb][:, 0:n], vecs[:, C_BO + m:C_BO + m + 1], xT[:, m, t0:t0 + n],
                        ALU.add, ALU.add, [R_ps[pb], R_x[m][tb], R_const], [R_x[m][tb]])
                    drain(3)
                pending.extend(post_tb(tb))
            drain()

        ysw = [win[h][:].bitcast(F32).rearrange("p a k n -> p (a k n)").rearrange("p (t d) -> p t d", t=2) for h in range(2)]
        fctr = [0]

        def final_pieces(tb):
            t0, n = TBS[tb]
            nt = n // 128
            pcs = norm_pieces(tb, 0)[:9]
            holder = {}

            def p_stt(c):
                def f():
                    i = (nctr[0] - 1) % 2
                    tq = c % 4
                    STT(T32[:, tq, 0:n], xT[:, c, t0:t0 + n], vecs[:, C_NFIN + c:C_NFIN + c + 1], rstd[:, i, 0:n],
                        ALU.mult, ALU.mult, [R_x[c][tb], R_rstd[i], R_const], [R_T32[tq]])
                return f

            def p_tr(cb):
                def f():
                    for cc in range(4):
                        tq = cc
                        for tt in range(nt):
                            TR(PS[cc][:, tt * 128:(tt + 1) * 128], T32[:, tq, tt * 128:(tt + 1) * 128], identf[:],
                               [R_T32[tq], R_const], [R_ps[cc]])
                return f

            def p_ev(cb, cc):
                def f():
                    c = cb * 4 + cc
                    for h in range((nt + 1) // 2):
                        w_ = min(2, nt - 2 * h)
                        src = PS[cc][:, h * 256:h * 256 + w_ * 128].rearrange("p (t d) -> p t d", t=w_)
                        dst = ysw[h][:, 0:w_, c * 128:(c + 1) * 128]
                        ACT(dst, src, AF.Copy, [R_ps[cc]], [R_win[h]])
                return f

            def p_out():
                if tb < 4:
                    for h in range(2):
                        DMA("sp", yp[t0 + h * 256:t0 + (h + 1) * 256, :].rearrange("(t p) d -> p t d", p=128), ysw[h],
                            [R_win[h]], [], D_out[h])
                else:
                    DMA("sp", ys, ysw[0][:, 0, :], [R_win[0]], [], D_out[0])

            seq_ = []
            for cb in range(2):
                seq_ += [p_stt(cb * 4 + cc) for cc in range(4)] + [p_tr(cb)] + [p_ev(cb, cc) for cc in range(4)]
            return pcs + seq_ + [p_out]

        pf = prefetch_first(0, 0)
        ffn(0, 0, C_NF1 + 0, pf, True, None)
        pf = prefetch_first(0, 1)
        pool_mixer(lambda tb: norm_pieces(tb, C_NF2 + 0))
        ffn(0, 1, C_NF2 + 0, pf, False, lambda tb: norm_pieces(tb, C_NF1 + 8))
        pf = prefetch_first(1, 0)
        ffn(1, 0, C_NF1 + 8, pf, False, lambda tb: norm_pieces(tb, C_NMIX + 8), mid_hook=lambda: attn_early_loads(0))
        pf_holder = {}
        attn_mixer(lambda tb: norm_pieces(tb, C_NF2 + 8))
        pf = pf_holder["pf"]
        ffn(1, 1, C_NF2 + 8, pf, False, final_pieces)
        P.emit(final_waits=[D_out[0], D_out[1], D_o2, D_d2d, D_ko, D_vo])
    return nc


def _host_consts():
    ident = np.eye(128, dtype=np.float32)
    k = np.arange(128)[:, None]
    q = np.arange(128)[None, :]
    masks = np.zeros((128, 3, 128), np.float32)
    masks[:, 0, :] = (k <= q)
    masks[:, 1, :] = (k > q)
    masks[:, 2, :] = ((k // 8) == (q // 8)) & ((k % 8) <= (q % 8))
    maskc = (np.arange(128)[:, None] > np.arange(8)[None, :]).astype(np.float32)
    half = 32
    inv = (np.float32(10000.0) ** (-np.arange(half, dtype=np.float32) / np.float32(half))).astype(np.float32)
    pos = np.concatenate([np.arange(2048), np.tile(8192 + np.arange(8), 16)]).astype(np.float32)
    ang = pos[:, None] * inv[None, :]
    cos = np.cos(ang).astype(np.float32)
    sin = np.sin(ang).astype(np.float32)
    d = np.arange(128) % 64
    cosT = cos[:, d % 32].T.copy()
    sgn = np.where(d < 32, -1.0, 1.0).astype(np.float32)
    sinT = (sin[:, d % 32].T * sgn[:, None]).astype(np.float32).copy()
    invc = np.zeros((128, 4, 16), np.float32)
    for g in range(4):
        w = 2 ** (g + 1)
        invc[:, g, :] = (1.0 / np.minimum(np.arange(16) + 1, w)).astype(np.float32)[None, :]
    return ident, masks, maskc, cosT, sinT, invc


def _fm(v):
    return np.ascontiguousarray(np.asarray(v, np.float32).reshape(8, 128).T)


_NC_CACHE = {}


def kernel(x_prompt, x_sample, state_pool, cache_k, cache_v,
           norm_ffn1, ffn1_w_in, ffn1_w_out, norm_mix, norm_ffn2, ffn2_w_in, ffn2_w_out,
           pool_w, pool_scale, attn_w_qkv, attn_b_qkv, attn_w_o, attn_b_o, attn_sinks,
           norm_final):
    f = lambda a: np.ascontiguousarray(np.asarray(a, dtype=np.float32))
    x_prompt, x_sample, state_pool = f(x_prompt), f(x_sample), f(state_pool)
    cache_k, cache_v = f(cache_k), f(cache_v)
    ident, masks, maskc, cosT, sinT, invc = _host_consts()

    wq = f(attn_w_qkv)[0]
    bq = f(attn_b_qkv)[0]
    perm64 = (np.arange(64) + 32) % 64
    cols = []
    bcols = []
    for j in range(2):
        for i in range(4):
            ha, hb = 8 * j + i, 8 * j + 4 + i
            o = np.concatenate([ha * 64 + np.arange(64), hb * 64 + np.arange(64)])
            pm = np.concatenate([ha * 64 + perm64, hb * 64 + perm64])
            cols += [o, pm]
            bcols += [o, pm]
        ko = 1024 + j * 128 + np.arange(128)
        kp = np.concatenate([1024 + (2 * j) * 64 + perm64, 1024 + (2 * j + 1) * 64 + perm64])
        cols += [ko, kp, 1280 + j * 128 + np.arange(128)]
        bcols += [ko, kp]
    wqkv_ext = np.ascontiguousarray(wq[:, np.concatenate(cols)])
    assert wqkv_ext.shape == (1024, 2816)
    rows = []
    for j in range(2):
        for i in range(4):
            ha, hb = 8 * j + i, 8 * j + 4 + i
            rows += [ha * 64 + np.arange(64), hb * 64 + np.arange(64)]
    wop = np.ascontiguousarray(f(attn_w_o)[0][np.concatenate(rows), :])

    vecs = np.zeros((128, NV), np.float32)
    for l in range(2):
        vecs[:, C_NF1 + l * 8:C_NF1 + l * 8 + 8] = _fm(norm_ffn1[l])
        vecs[:, C_NMIX + l * 8:C_NMIX + l * 8 + 8] = _fm(norm_mix[l])
        vecs[:, C_NF2 + l * 8:C_NF2 + l * 8 + 8] = _fm(norm_ffn2[l])
    vecs[:, C_NFIN:C_NFIN + 8] = _fm(norm_final)
    vecs[:, C_PSC:C_PSC + 8] = _fm(f(pool_scale)[0])
    vecs[:, C_BO:C_BO + 8] = _fm(f(attn_b_o)[0])
    for idx, bc in enumerate(bcols):
        vecs[:, C_BQ + idx] = bq[bc]
    bvb = np.ascontiguousarray(np.broadcast_to(bq[1280:1536].reshape(1, 2, 128), (128, 2, 128))).astype(np.float32)
    sinks = np.ascontiguousarray(np.broadcast_to(f(attn_sinks).reshape(1, 16), (2, 16))).astype(np.float32)
    sel = np.eye(2, dtype=np.float32)
    pidx = (np.arange(128) // 64) * 64 + ((np.arange(128) % 64) + 32) % 64
    permm = np.zeros((128, 128), np.float32)
    permm[pidx, np.arange(128)] = 1.0

    shared = {
        "w1i": f(ffn1_w_in), "w2i": f(ffn2_w_in), "w1o": f(ffn1_w_out), "w2o": f(ffn2_w_out),
        "pw": f(pool_w)[0], "wqkv": wqkv_ext, "wop": wop, "vecs": vecs, "ident": ident, "masks": masks,
        "maskc": maskc, "cosT": cosT, "sinT": sinT, "invc": invc, "bvb": bvb, "sinks": sinks, "sel": sel, "permm": permm,
    }
    in_maps = []
    for c in range(8):
        m = dict(shared)
        m["xp"] = x_prompt[c]
        m["xs"] = np.ascontiguousarray(x_sample[16 * c:16 * c + 16].reshape(128, D))
        m["spool"] = np.ascontiguousarray(state_pool[0, 16 * c:16 * c + 16])
        m["ck"] = np.ascontiguousarray(cache_k[0, 16 * c:16 * c + 16].reshape(16, 128, 256))
        m["cv"] = np.ascontiguousarray(cache_v[0, 16 * c:16 * c + 16].reshape(16, 128, 256))
        in_maps.append(m)

    if "nc" not in _NC_CACHE:
        _NC_CACHE["nc"] = build_program()
    nc = _NC_CACHE["nc"]
    res = run_bass_kernel_spmd(nc, in_maps, core_ids=list(range(8)))
    r = res.results
    y_prompt = np.stack([r[c]["yp"] for c in range(8)], 0)
    y_sample = np.concatenate([r[c]["ys"].reshape(16, 8, D) for c in range(8)], 0)
    new_pool_prompt = np.stack([r[c]["npp"] for c in range(8)], 0)[None]
    new_pool_sample = np.concatenate([r[c]["nps"] for c in range(8)], 0)[None]
    new_k_prompt = np.stack([r[c]["nkp"].reshape(128, 4, 64) for c in range(8)], 0)[None]
    new_v_prompt = np.stack([r[c]["nvp"].reshape(128, 4, 64) for c in range(8)], 0)[None]
    new_k_sample = np.concatenate([r[c]["nks"].reshape(16, 128, 4, 64) for c in range(8)], 0)[None]
    new_v_sample = np.concatenate([r[c]["nvs"].reshape(16, 128, 4, 64) for c in range(8)], 0)[None]
    return (y_prompt.astype(np.float32), y_sample.astype(np.float32), new_pool_prompt.astype(np.float32),
            new_pool_sample.astype(np.float32), new_k_prompt.astype(np.float32), new_v_prompt.astype(np.float32),
            new_k_sample.astype(np.float32), new_v_sample.astype(np.float32))
```
